# Optimizing a Trainium2 kernel written in Bass

```python
import math
import jax, jax.numpy as jnp
from jax import lax
import numpy as np

D_MODEL = 1024
BATCH = 16
SEQ = 2048
DEPTH = 2

GRID_W = 64
CTX_LEN = 256
EPS = 1e-6
ROPE_BASE = 10000.0
Q_BLOCK = 128
MLA_HEADS = 8
MLA_NOPE = 64
MLA_ROPE = 32
MLA_V = 64
Q_LORA = 256
KV_LORA = 128
MLA_WIDTH = MLA_HEADS * MLA_V
LRU_HEADS = 4
LRU_WIDTH = 256
LRU_BLOCK = LRU_WIDTH // LRU_HEADS
CONV_W = 4
LRU_C = 8.0
SGU_GROUPS = 4
SGU_WIDTH = 256
SGU_GROUP_DIM = SGU_WIDTH // SGU_GROUPS
CHUNK = 128
MIX_WIDTH = MLA_WIDTH + LRU_WIDTH + SGU_WIDTH
IN_SIZES = (Q_LORA, KV_LORA, MLA_ROPE, LRU_WIDTH, LRU_WIDTH, SGU_WIDTH, SGU_WIDTH)
IN_WIDTH = sum(IN_SIZES)
IN_SPLITS = tuple(int(s) for s in np.cumsum(IN_SIZES)[:-1])
D_FF_RAW = -(-8 * D_MODEL // 3)
D_FF = (D_FF_RAW + 255) // 256 * 256

kernel_name = 'hybrid_mla_rglru_sgu_dit_block'

F32 = jnp.float32


def rms_norm(x, g):
    xf = x.astype(F32)
    y = xf * lax.rsqrt(jnp.mean(xf * xf, axis=-1, keepdims=True) + EPS)
    return (y * g.astype(F32)).astype(x.dtype)


def modulate(h, shift, scale):
    return h * (1 + scale[:, None]) + shift[:, None]


def rope_2d_tables(seq_len):
    rows = seq_len // GRID_W
    row = jnp.repeat(jnp.arange(rows, dtype=F32), GRID_W)
    col = jnp.tile(jnp.arange(GRID_W, dtype=F32), rows)
    half = MLA_ROPE // 2
    freq = ROPE_BASE ** (-jnp.arange(0, half, 2, dtype=F32) / half)
    ar = row[:, None] * freq
    ac = col[:, None] * freq
    ang = jnp.concatenate([ar, ar, ac, ac], axis=-1)
    return jnp.cos(ang)[:, None, :], jnp.sin(ang)[:, None, :]


def _rot_half(v):
    v1, v2 = jnp.split(v, 2, axis=-1)
    return jnp.concatenate([-v2, v1], axis=-1)


def apply_rope_2d(x, cos, sin):
    xr, xc = jnp.split(x, 2, axis=-1)
    rotated = jnp.concatenate([_rot_half(xr), _rot_half(xc)], axis=-1)
    return x * cos.astype(x.dtype) + rotated * sin.astype(x.dtype)


def block_attention(q, k, v):
    B, S, H, dq = q.shape
    nb = S // Q_BLOCK
    qb = jnp.moveaxis(q.reshape(B, nb, Q_BLOCK, H, dq), 1, 0)
    scale = 1.0 / math.sqrt(dq)

    def one_block(qi):
        s = jnp.einsum('bqhd,bkhd->bhqk', qi, k).astype(F32) * scale
        p = jax.nn.softmax(s, axis=-1).astype(v.dtype)
        return jnp.einsum('bhqk,bkhd->bqhd', p, v)

    o = lax.map(one_block, qb)
    return jnp.moveaxis(o, 0, 1).reshape(B, S, H * v.shape[-1])


def mla_q(qa, g, w):
    B, T, _ = qa.shape
    return (rms_norm(qa, g) @ w).reshape(B, T, MLA_HEADS, MLA_NOPE + MLA_ROPE)


def mla_kv(kva, k_rope, g, w):
    B, T, _ = kva.shape
    kv = (rms_norm(kva, g) @ w).reshape(B, T, MLA_HEADS, MLA_NOPE + MLA_V)
    k_nope, v = kv[..., :MLA_NOPE], kv[..., MLA_NOPE:]
    k = jnp.concatenate([k_nope, jnp.broadcast_to(k_rope, (B, T, MLA_HEADS, MLA_ROPE))], axis=-1)
    return k, v


def short_conv(x, w, b):
    C = x.shape[-1]
    y = lax.conv_general_dilated(
        x, w[:, None, :].astype(x.dtype), window_strides=(1,),
        padding=[(CONV_W // 2, CONV_W - 1 - CONV_W // 2)],
        dimension_numbers=('NWC', 'WIO', 'NWC'), feature_group_count=C)
    return y + b


def lru_gates(xc, w_r, b_r, w_i, b_i, lam):
    B, T, C = xc.shape
    xb = xc.reshape(B, T, LRU_HEADS, LRU_BLOCK)
    r = jax.nn.sigmoid(jnp.einsum('bthi,hij->bthj', xb, w_r).reshape(B, T, C) + b_r).astype(F32)
    i = jax.nn.sigmoid(jnp.einsum('bthi,hij->bthj', xb, w_i).reshape(B, T, C) + b_i)
    log_a = -LRU_C * r * jax.nn.softplus(-lam.astype(F32))
    a = jnp.exp(log_a)
    b = jnp.sqrt(-jnp.expm1(2.0 * log_a)) * (i * xc).astype(F32)
    return a, b


def _lin_combine(left, right):
    a_l, b_l = left
    a_r, b_r = right
    return a_l * a_r, a_r * b_l + b_r


def linear_scan(a, b, h0, reverse):
    if reverse:
        a = jnp.flip(a, axis=1)
        b = jnp.flip(b, axis=1)
    b = b.at[:, 0].add(a[:, 0] * h0)
    _, h = lax.associative_scan(_lin_combine, (a, b), axis=1)
    return jnp.flip(h, axis=1) if reverse else h


def sgu(u, v, g, w_s, b_s):
    B, T, _ = u.shape
    n = T // CHUNK
    u = jax.nn.gelu(u)
    vg = jax.nn.gelu(v).reshape(B, n, CHUNK, SGU_GROUPS, SGU_GROUP_DIM)
    vg = rms_norm(vg, g.reshape(SGU_GROUPS, SGU_GROUP_DIM))
    s = jnp.einsum('gpq,bnqgc->bnpgc', w_s, vg) + b_s.T[None, None, :, :, None]
    return u * s.reshape(B, T, SGU_WIDTH)


def merge_groups(att, rec, sp, g, w_out):
    y = jnp.concatenate([
        rms_norm(att, g[:MLA_WIDTH]),
        rms_norm(rec, g[MLA_WIDTH:MLA_WIDTH + LRU_WIDTH]),
        rms_norm(sp, g[MLA_WIDTH + LRU_WIDTH:])], axis=-1)
    return y @ w_out


def token_mixers(h_l, h_c, cos, sin, w_in, q_a_norm, w_q_b, kv_a_norm, w_kv_b, conv_w, conv_b,
                 lru_w_r, lru_b_r, lru_w_i, lru_b_i, lru_lam, sgu_norm, sgu_w, sgu_b,
                 out_norm, w_out, need_ctx):
    B = h_l.shape[0]
    qa_l, kva_l, kr_l, xr_l, gr_l, su_l, sv_l = jnp.split(h_l @ w_in, IN_SPLITS, axis=-1)
    qa_c, kva_c, kr_c, xr_c, gr_c, su_c, sv_c = jnp.split(h_c @ w_in, IN_SPLITS, axis=-1)

    k_l, v_l = mla_kv(kva_l, apply_rope_2d(kr_l[:, :, None, :], cos, sin), kv_a_norm, w_kv_b)
    k_c, v_c = mla_kv(kva_c, kr_c[:, :, None, :], kv_a_norm, w_kv_b)
    q_l = mla_q(qa_l, q_a_norm, w_q_b)
    q_l = jnp.concatenate([q_l[..., :MLA_NOPE], apply_rope_2d(q_l[..., MLA_NOPE:], cos, sin)], axis=-1)
    att_l = block_attention(q_l, jnp.concatenate([k_c, k_l], axis=1), jnp.concatenate([v_c, v_l], axis=1))

    xc_l = short_conv(xr_l, conv_w, conv_b)
    xc_c = short_conv(xr_c, conv_w, conv_b)
    zero = jnp.zeros((B, LRU_WIDTH), F32)
    hsum_l = jnp.zeros(xc_l.shape, F32)
    hsum_c = jnp.zeros(xc_c.shape, F32)
    for d, rev in ((0, False), (1, True)):
        a_c, b_c = lru_gates(xc_c, lru_w_r[d], lru_b_r[d], lru_w_i[d], lru_b_i[d], lru_lam[d])
        a_l, b_l = lru_gates(xc_l, lru_w_r[d], lru_b_r[d], lru_w_i[d], lru_b_i[d], lru_lam[d])
        hc = linear_scan(a_c, b_c, zero, rev)
        seed = hc[:, 0] if rev else hc[:, -1]
        hl = linear_scan(a_l, b_l, seed, rev)
        hsum_l = hsum_l + hl
        hsum_c = hsum_c + hc
    rec_l = hsum_l.astype(h_l.dtype) * jax.nn.gelu(gr_l)

    sp_l = sgu(su_l, sv_l, sgu_norm, sgu_w, sgu_b)

    y_l = merge_groups(att_l, rec_l, sp_l, out_norm, w_out)
    if not need_ctx:
        return y_l, None
    q_c = mla_q(qa_c, q_a_norm, w_q_b)
    att_c = block_attention(q_c, k_c, v_c)
    rec_c = hsum_c.astype(h_c.dtype) * jax.nn.gelu(gr_c)
    sp_c = sgu(su_c, sv_c, sgu_norm, sgu_w, sgu_b)
    y_c = merge_groups(att_c, rec_c, sp_c, out_norm, w_out)
    return y_l, y_c


def swiglu(h, w_in, w_out):
    g, u = jnp.split(h @ w_in, 2, axis=-1)
    return (jax.nn.silu(g) * u) @ w_out


def setup_inputs(seed: int = 0) -> dict:
    key = jax.random.key(seed)
    ks = jax.random.split(key, 28)

    def nrm(k, shape, fan_in, scale=1.0):
        return scale * fan_in ** -0.5 * jax.random.normal(k, shape, F32)

    def gain(k, shape):
        return 1.0 + 0.01 * jax.random.normal(k, shape, F32)

    a0 = jax.random.uniform(ks[19], (DEPTH, 2, LRU_WIDTH), F32, 0.9, 0.999)
    s0 = a0 ** (1.0 / LRU_C)
    lam = jnp.log(s0) - jnp.log1p(-s0)
    return {
        'x': jax.random.normal(ks[0], (BATCH, SEQ, D_MODEL), F32),
        'c': jax.random.normal(ks[1], (BATCH, D_MODEL), F32),
        'ctx': jax.random.normal(ks[2], (BATCH, CTX_LEN, D_MODEL), F32),
        'c_ctx': jax.random.normal(ks[3], (D_MODEL,), F32),
        'norm1': gain(ks[4], (DEPTH, D_MODEL)),
        'norm2': gain(ks[5], (DEPTH, D_MODEL)),
        'w_ada': nrm(ks[6], (DEPTH, D_MODEL, 6 * D_MODEL), D_MODEL, 0.3),
        'b_ada': 0.01 * jax.random.normal(ks[7], (DEPTH, 6 * D_MODEL), F32),
        'w_in': nrm(ks[8], (DEPTH, D_MODEL, IN_WIDTH), D_MODEL),
        'q_a_norm': gain(ks[9], (DEPTH, Q_LORA)),
        'w_q_b': nrm(ks[10], (DEPTH, Q_LORA, MLA_HEADS * (MLA_NOPE + MLA_ROPE)), Q_LORA),
        'kv_a_norm': gain(ks[11], (DEPTH, KV_LORA)),
        'w_kv_b': nrm(ks[12], (DEPTH, KV_LORA, MLA_HEADS * (MLA_NOPE + MLA_V)), KV_LORA),
        'conv_w': nrm(ks[13], (DEPTH, CONV_W, LRU_WIDTH), CONV_W),
        'conv_b': 0.01 * jax.random.normal(ks[14], (DEPTH, LRU_WIDTH), F32),
        'lru_w_r': nrm(ks[15], (DEPTH, 2, LRU_HEADS, LRU_BLOCK, LRU_BLOCK), LRU_BLOCK),
        'lru_b_r': 0.01 * jax.random.normal(ks[16], (DEPTH, 2, LRU_WIDTH), F32),
        'lru_w_i': nrm(ks[17], (DEPTH, 2, LRU_HEADS, LRU_BLOCK, LRU_BLOCK), LRU_BLOCK),
        'lru_b_i': 0.01 * jax.random.normal(ks[18], (DEPTH, 2, LRU_WIDTH), F32),
        'lru_lam': lam,
        'sgu_norm': gain(ks[20], (DEPTH, SGU_WIDTH)),
        'sgu_w': nrm(ks[21], (DEPTH, SGU_GROUPS, CHUNK, CHUNK), CHUNK),
        'sgu_b': 1.0 + 0.01 * jax.random.normal(ks[22], (DEPTH, SGU_GROUPS, CHUNK), F32),
        'out_norm': gain(ks[23], (DEPTH, MIX_WIDTH)),
        'w_out': nrm(ks[24], (DEPTH, MIX_WIDTH, D_MODEL), MIX_WIDTH),
        'w_ffn_in': nrm(ks[25], (DEPTH, D_MODEL, 2 * D_FF), D_MODEL),
        'w_ffn_out': nrm(ks[26], (DEPTH, D_FF, D_MODEL), D_FF),
        'final_norm': gain(ks[27], (D_MODEL,)),
    }


def reference(x, c, ctx, c_ctx, norm1, norm2, w_ada, b_ada, w_in, q_a_norm, w_q_b, kv_a_norm, w_kv_b,
              conv_w, conv_b, lru_w_r, lru_b_r, lru_w_i, lru_b_i, lru_lam, sgu_norm, sgu_w, sgu_b,
              out_norm, w_out, w_ffn_in, w_ffn_out, final_norm):
    cos, sin = rope_2d_tables(x.shape[1])
    h_ctx = ctx
    for l in range(DEPTH):
        last = l == DEPTH - 1
        mod_l = jax.nn.silu(c) @ w_ada[l] + b_ada[l]
        mod_c = (jax.nn.silu(c_ctx) @ w_ada[l] + b_ada[l])[None]
        sh1, sc1, g1, sh2, sc2, g2 = jnp.split(mod_l, 6, axis=-1)
        csh1, csc1, cg1, csh2, csc2, cg2 = jnp.split(mod_c, 6, axis=-1)

        hl = modulate(rms_norm(x, norm1[l]), sh1, sc1)
        hc = modulate(rms_norm(h_ctx, norm1[l]), csh1, csc1)
        y_l, y_c = token_mixers(hl, hc, cos, sin, w_in[l], q_a_norm[l], w_q_b[l], kv_a_norm[l], w_kv_b[l],
                                conv_w[l], conv_b[l], lru_w_r[l], lru_b_r[l], lru_w_i[l], lru_b_i[l],
                                lru_lam[l], sgu_norm[l], sgu_w[l], sgu_b[l], out_norm[l], w_out[l],
                                not last)
        x = x + g1[:, None] * y_l
        x = x + g2[:, None] * swiglu(modulate(rms_norm(x, norm2[l]), sh2, sc2), w_ffn_in[l], w_ffn_out[l])
        if not last:
            h_ctx = h_ctx + cg1[:, None] * y_c
            h_ctx = h_ctx + cg2[:, None] * swiglu(modulate(rms_norm(h_ctx, norm2[l]), csh2, csc2),
                                                 w_ffn_in[l], w_ffn_out[l])
    return rms_norm(x, final_norm)
```

```python
import numpy as np
from contextlib import ExitStack
import concourse.bass as bass
import concourse.mybir as mybir
from concourse.bass_utils import run_bass_kernel_spmd

F32 = mybir.dt.float32
BF16 = mybir.dt.bfloat16
AF = mybir.ActivationFunctionType
ALU = mybir.AluOpType
AX = mybir.AxisListType

NCORES = 8
D = 1024
SEQ = 2048
CTX = 256
TOK = SEQ + CTX
DEPTH = 2
DFF = 2816
NJ = DFF // 128
EPS = 1e-6
TILES = [(0, 256), (256, 512), (768, 512), (1280, 512), (1792, 512)]
NDMA = 40
NSWDMA = 40
GELU_C = 1.5957691216057308

WNAMES = [("win", 768), ("wq", 128), ("wkv", 64), ("wout", 512), ("wgu", 2816), ("wo2", 1408), ("lru", 64), ("sgw", 32)]
WROWS = sum(r for _, r in WNAMES)
WOFF = {}
_o = 0
for _n, _r in WNAMES:
    WOFF[_n] = (_o, _r)
    _o += _r
ADA_ROWS = 3072

VEC_FIELDS = [("norm1", 8), ("norm2", 8), ("b_ada", 48), ("qn_g", 2), ("kvn_g", 1), ("conv_w", 8), ("conv_b", 2),
              ("lru_b_r", 4), ("lru_b_i", 4), ("lru_lam", 4), ("out_norm", 8), ("sgu_b", 4)]
VOFF = {}
_o = 0
for _l in range(DEPTH):
    for _n, _w in VEC_FIELDS:
        VOFF[(_n, _l)] = _o
        _o += _w
VOFF["final"] = _o
_o += 8
VOFF["cT"] = _o
_o += 24
NV = _o

ROPE_PERM = np.array(list(range(8, 16)) + list(range(0, 8)) + list(range(24, 32)) + list(range(16, 24)))
ROPE_SIGN = np.array([-1.0] * 8 + [1.0] * 8 + [-1.0] * 8 + [1.0] * 8, np.float32)


def _pc(v, nchunk):
    return np.ascontiguousarray(v.reshape(nchunk, 128).T)


def _rope_tables():
    rows = SEQ // 64
    row = np.repeat(np.arange(rows, dtype=np.float32), 64)
    col = np.tile(np.arange(64, dtype=np.float32), rows)
    half = 16
    freq = (np.float32(10000.0) ** (-np.arange(0, half, 2, dtype=np.float32) / np.float32(half))).astype(np.float32)
    ar = row[:, None] * freq
    ac = col[:, None] * freq
    ang = np.concatenate([ar, ar, ac, ac], axis=-1).astype(np.float32)
    cos = np.cos(ang).astype(np.float32)
    sin = np.sin(ang).astype(np.float32) * ROPE_SIGN[None, :]
    t = np.zeros((128, 2, SEQ), np.float32)
    t[64:96, 0, :] = cos.T
    t[64:96, 1, :] = sin.T
    return t


def _layer_weights(inp, l):
    w_in = inp["w_in"][l]
    kr = w_in[:, 384:416]
    g1 = np.concatenate([w_in[:, 0:256], w_in[:, 256:384], kr, kr[:, ROPE_PERM], np.zeros((D, 64), np.float32)], 1)
    g2 = w_in[:, 416:928]
    g3 = w_in[:, 928:1440]
    win = np.concatenate([g1, g2, g3], 1).reshape(8, 128, 1536).transpose(1, 0, 2)
    wqb = inp["w_q_b"][l].reshape(256, 8, 96)
    wq = np.concatenate([wqb, wqb[:, :, 64 + ROPE_PERM]], 2).reshape(2, 128, 8, 128).transpose(1, 0, 2, 3)
    wkvb = inp["w_kv_b"][l].reshape(128, 8, 128)
    wkv = np.concatenate([wkvb[:, :, :64].reshape(128, 512), wkvb[:, :, 64:].reshape(128, 512)], 1)
    wout = inp["w_out"][l].reshape(8, 128, 8, 128).transpose(2, 1, 0, 3)
    wfi = inp["w_ffn_in"][l].reshape(8, 128, 2, NJ, 128)
    wgu = wfi.transpose(3, 1, 0, 2, 4)
    wo2 = inp["w_ffn_out"][l].reshape(NJ, 128, 8, 128).transpose(2, 1, 0, 3)
    lru = np.zeros((128, 2, 2, 2, 128), np.float32)
    for ri, nm in enumerate(("lru_w_r", "lru_w_i")):
        w = inp[nm][l]
        for d in range(2):
            for c in range(2):
                for hh in range(2):
                    lru[hh * 64:(hh + 1) * 64, ri, d, c, hh * 64:(hh + 1) * 64] = w[d, 2 * c + hh]
    sgw = inp["sgu_w"][l].transpose(2, 0, 1)
    parts = [win, wq, wkv, wout, wgu, wo2, lru, sgw]
    flat = np.concatenate([np.ascontiguousarray(p, dtype=np.float32).reshape(-1) for p in parts])
    assert flat.size == WROWS * 2048
    return flat.reshape(WROWS, 2048)


def _vecs(inp, cvecs):
    v = np.zeros((128, NV), np.float32)
    for l in range(DEPTH):
        def put(name, arr):
            o = VOFF[(name, l)]
            v[:, o:o + arr.shape[1]] = arr
        put("norm1", _pc(inp["norm1"][l], 8))
        put("norm2", _pc(inp["norm2"][l], 8))
        put("b_ada", _pc(inp["b_ada"][l], 48))
        put("qn_g", _pc(inp["q_a_norm"][l], 2))
        put("kvn_g", _pc(inp["kv_a_norm"][l], 1))
        cw = inp["conv_w"][l]
        put("conv_w", np.ascontiguousarray(cw.reshape(4, 2, 128).transpose(2, 1, 0)).reshape(128, 8))
        put("conv_b", _pc(inp["conv_b"][l], 2))
        for nm, key in (("lru_b_r", "lru_b_r"), ("lru_b_i", "lru_b_i"), ("lru_lam", "lru_lam")):
            a = inp[key][l]
            put(nm, np.ascontiguousarray(a.reshape(2, 2, 128).transpose(2, 0, 1)).reshape(128, 4))
        put("out_norm", _pc(inp["out_norm"][l], 8))
        put("sgu_b", np.ascontiguousarray(inp["sgu_b"][l].T))
    v[:, VOFF["final"]:VOFF["final"] + 8] = _pc(inp["final_norm"], 8)
    cT = np.stack([_pc(cv, 8) for cv in cvecs], axis=2)
    v[:, VOFF["cT"]:VOFF["cT"] + 24] = cT.reshape(128, 24)
    return v


ENG = ["pe", "act", "dve", "pool", "sp"]
DBG = {"norr_sgu": 1}


class Buf:
    __slots__ = ("w", "r", "name")

    def __init__(self, name=""):
        self.w = None
        self.r = {}
        self.name = name


class Sched:
    def __init__(self, nc, marks=None):
        self.nc = nc
        self.marks = None if marks is None else {e: {idx: i + 1 for i, idx in enumerate(sorted(marks[e]))} for e in marks}
        self.used = {e: set() for e in ENG}
        self.eng = {"pe": nc.tensor, "act": nc.scalar, "dve": nc.vector, "pool": nc.gpsimd, "sp": nc.sync}
        self.cnt = {e: 0 for e in ENG}
        self.known = {e: {} for e in ENG}
        self.root = ExitStack()
        self.esem = {e: self.root.enter_context(nc.semaphore("s_" + e)) for e in ENG}
        self.dsem = [self.root.enter_context(nc.semaphore("d%d" % i)) for i in range(NDMA + NSWDMA)]
        self.dcnt = [0] * (NDMA + NSWDMA)
        self.dbar = [0] * (NDMA + NSWDMA)
        self.swnext = NDMA
        self.dnext = 0
        self.scopes = [self.root]
        self.nid = 0
        self.ninstr = 0

    def sb(self, shape, dtype, name=None):
        self.nid += 1
        return self.scopes[-1].enter_context(self.nc.sbuf_tensor("%s_%d" % (name or "t", self.nid), list(shape), dtype))

    def psum(self, shape, dtype, name=None):
        self.nid += 1
        return self.root.enter_context(self.nc.psum_tensor("%s_%d" % (name or "ps", self.nid), list(shape), dtype))

    def push(self):
        st = ExitStack()
        self.scopes.append(st)

    def pop(self):
        self.barrier()
        self.scopes.pop().close()

    def _sem(self, k):
        return self.esem[k] if isinstance(k, str) else self.dsem[k[1]]

    def _waits(self, e, needs):
        kn = self.known[e]
        for k, v in needs.items():
            if kn.get(k, 0) >= v:
                continue
            kn[k] = v
            val = v
            if isinstance(k, str):
                self.used[k].add(v)
                if self.marks is not None:
                    val = self.marks[k][v]
            self.eng[e].wait_ge(self._sem(k), val)
            self.ninstr += 1

    def op(self, e, fn, reads=(), writes=()):
        needs = {}
        for b in reads:
            if b.w is not None and needs.get(b.w[0], 0) < b.w[1]:
                needs[b.w[0]] = b.w[1]
        strict = e != "pe"
        for b in writes:
            if b.w is not None and (strict or b.w[0] != e) and needs.get(b.w[0], 0) < b.w[1]:
                needs[b.w[0]] = b.w[1]
            for k, v in b.r.items():
                if (strict or k != e) and needs.get(k, 0) < v:
                    needs[k] = v
        self._waits(e, needs)
        ins = fn(self.eng[e])
        self.cnt[e] += 1
        v = self.cnt[e]
        if self.marks is None or v in self.marks[e]:
            ins.then_inc(self.esem[e], 1)
        self.ninstr += 1
        for b in reads:
            b.r[e] = v
        for b in writes:
            b.w = (e, v)
            b.r = {}
        return ins

    def dma(self, q, out, in_, reads=(), writes=(), nobar=False, **kw):
        if q == "pool":
            slot = self.swnext
            self.swnext += 1
            assert slot < NDMA + NSWDMA
        else:
            slot = self.dnext
            self.dnext = (self.dnext + 1) % NDMA
        key = ("d", slot)
        needs = {}
        if self.dcnt[slot]:
            needs[key] = 16 * self.dcnt[slot]
        for b in reads:
            if b.w is not None and needs.get(b.w[0], 0) < b.w[1]:
                needs[b.w[0]] = b.w[1]
        for b in writes:
            if b.w is not None and needs.get(b.w[0], 0) < b.w[1]:
                needs[b.w[0]] = b.w[1]
            for k, v in b.r.items():
                if needs.get(k, 0) < v:
                    needs[k] = v
        self._waits(q, needs)
        self.eng[q].dma_start(out=out, in_=in_, **kw).then_inc(self.dsem[slot], 16)
        self.ninstr += 1
        self.dcnt[slot] += 1
        v = 16 * self.dcnt[slot]
        if not nobar:
            self.dbar[slot] = self.dcnt[slot]
        for b in reads:
            b.r[key] = v
        for b in writes:
            b.w = (key, v)
            b.r = {}

    def barrier(self, final=False):
        for e in ENG:
            needs = {k: self.cnt[k] for k in ENG if k != e and self.cnt[k] > 0}
            for s in range(NDMA + NSWDMA):
                n_ = self.dcnt[s] if final else self.dbar[s]
                if n_:
                    needs[("d", s)] = 16 * n_
            self._waits(e, needs)

    def finish(self):
        self.barrier(final=True)
        self.root.close()


def build_program(stage=99, taps=(), marks=None):
    nc = bass.Bass("TRN2", target_bir_lowering=False)
    S = Sched(nc, marks)
    op, dma = S.op, S.dma

    xT_in = nc.dram_tensor("xT", [2, 128, 8, TOK], F32, kind="ExternalInput").ap()
    wl_in = [nc.dram_tensor("wl%d" % l, [WROWS, 2048], F32, kind="ExternalInput").ap() for l in range(DEPTH)]
    wada_in = nc.dram_tensor("wada", [DEPTH * ADA_ROWS, 2048], F32, kind="ExternalInput").ap()
    vecs_in = nc.dram_tensor("vecs", [128, NV], F32, kind="ExternalInput").ap()
    sgn_in = nc.dram_tensor("sgn", [128, DEPTH * 256], F32, kind="ExternalInput").ap()
    rope_in = nc.dram_tensor("rope", [128, 2, SEQ], F32, kind="ExternalInput").ap()
    ident_in = nc.dram_tensor("ident", [128, 128], F32, kind="ExternalInput").ap()
    outT = nc.dram_tensor("outT", [2, 128, 8, SEQ], F32, kind="ExternalOutput").ap()
    wb = [nc.dram_tensor("wb%d" % l, [WROWS, 2048], BF16, kind="Internal").ap() for l in range(DEPTH)]
    xs = nc.dram_tensor("xs", [2, 128, 8, TOK], F32, kind="Internal").ap()
    tap_out = {}
    for name, shape in taps:
        tap_out[name] = nc.dram_tensor("tap_" + name, list(shape), F32, kind="ExternalOutput").ap()

    wbuf = {(l, n): Buf("wb%d_%s" % (l, n)) for l in range(DEPTH) for n, _ in WNAMES}
    adabuf = [Buf("ada%d" % l) for l in range(DEPTH)]
    xbuf = {(b, ti): Buf("x%d_%d" % (b, ti)) for b in range(2) for ti in range(len(TILES))}
    outbuf = Buf("out")

    def wview(l, name, pattern, **kw):
        o, r = WOFF[name]
        return wb[l][o:o + r, :].rearrange("r c -> (r c)").rearrange(pattern, **kw)

    bigs = [S.psum([128, 1024], F32, "big") for _ in range(2)]
    bigb = [Buf("big0"), Buf("big1")]
    banks = [bigs[0][:, 0:512], bigs[0][:, 512:1024], bigs[1][:, 0:512], bigs[1][:, 512:1024]]
    banks += [S.psum([128, 512], F32, "bank") for _ in range(3)]
    bankb = [Buf("bank%d" % i) for i in range(7)]
    tbank = S.psum([128, 1024], BF16, "tbank")
    tbankb = Buf("tbank")
    banks.append(tbank[:].bitcast(F32))
    bankb.append(tbankb)
    pool_ids = [0, 1, 2, 3, 4]
    rr = [0]

    def ps():
        i = pool_ids[rr[0] % len(pool_ids)]
        rr[0] += 1
        return banks[i], bankb[i]

    vecs = S.sb([128, NV], F32, "vecs")
    vecsb = Buf("vecs")
    sgn = S.sb([128, DEPTH * 256], F32, "sgn")
    ident_f = S.sb([128, 128], F32, "identf")
    ident = S.sb([128, 128], BF16, "ident")
    ones = S.sb([128, 128], BF16, "ones")
    modv = S.sb([128, DEPTH, 48, 3], F32, "modv")
    gmv = S.sb([128, DEPTH, 2, 8, 3], F32, "gmv")
    clv = S.sb([128, DEPTH, 8], F32, "clv")
    constb = Buf("const")
    dma("sp", vecs[:], vecs_in[:, :], writes=[vecsb])
    dma("sp", sgn[:], sgn_in[:, :], writes=[constb])
    dma("sp", ident_f[:], ident_in[:, :], writes=[constb])
    op("dve", lambda e: e.tensor_copy(out=ident[:], in_=ident_f[:]), reads=[constb], writes=[constb])
    op("pool", lambda e: e.memset(ones[:], 1.0), writes=[constb])

    def vec(name, l=None, i=0, n=1):
        o = VOFF[name] if l is None else VOFF[(name, l)]
        return vecs[:, o + i:o + i + n]

    def cast_rows(dst, src, r0, r1, buf):
        r = r0
        while r < r1:
            n = min(1024, r1 - r)
            dma("pool", dst[r:r + n, :], src[r:r + n, :], writes=[buf], nobar=True)
            r += n

    def cast_weights(l_, names):
        for n in names:
            o, r = WOFF[n]
            cast_rows(wb[l_], wl_in[l_], o, o + r, wbuf[(l_, n)])

    cast_weights(0, ("win", "wkv", "sgw"))

    modb = Buf("mod")

    class ModJob:
        def __init__(self, l):
            self.l = l

        def start(self):
            l = self.l
            self.scT = S.sb([128, 8, 3], F32, "scT")
            self.scb = Buf("scT")
            cTv = vecs[:, VOFF["cT"]:VOFF["cT"] + 24]
            op("act", lambda e: e.activation(out=self.scT[:].rearrange("p k r -> p (k r)"), in_=cTv, func=AF.Silu), reads=[vecsb], writes=[self.scb])
            self.was = [S.sb([128, 8, 256], F32, "wada") for _ in range(2)]
            self.wabs = [Buf("wa%d" % i) for i in range(2)]
            self.mtm = S.sb([3, 256], F32, "mtm")
            self.mtmb = Buf("mtm")
            self.src = wada_in[l * ADA_ROWS:(l + 1) * ADA_ROWS, :].rearrange("r c -> (r c)").rearrange("(p k n) -> p k n", p=128, k=8)

        def group(self, g):
            l = self.l
            wa, wab = self.was[g % 2], self.wabs[g % 2]
            dma("sp", wa[:], self.src[:, :, g * 256:(g + 1) * 256], writes=[wab])
            bk, bb = ps()
            for kc in range(8):
                op("pe", lambda e: e.matmul(bk[0:3, 0:256], lhsT=self.scT[:, kc, :], rhs=wa[:, kc, :], start=(kc == 0), stop=(kc == 7)), reads=[wab, self.scb], writes=[bb])
            op("act", lambda e: e.activation(out=self.mtm[0:3, :], in_=bk[0:3, 0:256], func=AF.Copy), reads=[bb], writes=[self.mtmb])
            bt, btb = ps()
            for m in range(2):
                op("pe", lambda e: e.transpose(bt[:, m * 3:(m + 1) * 3], self.mtm[0:3, m * 128:(m + 1) * 128], ident_f[0:3, 0:3]), reads=[self.mtmb, constb], writes=[btb])
            o = VOFF[("b_ada", l)] + 2 * g
            op("dve", lambda e: e.tensor_tensor(out=modv[:, l, 2 * g:2 * g + 2, :], in0=bt[:, 0:6].rearrange("p (m r) -> p m r", r=3),
                                                in1=vecs[:, o:o + 2].unsqueeze(2).to_broadcast([128, 2, 3]), op=ALU.add), reads=[btb, vecsb], writes=[modb])

        def finish(self):
            l = self.l
            t1 = S.sb([128, 4], F32, "t1")
            t2 = S.sb([128, 4], F32, "t2")
            t3 = S.sb([128, 4], F32, "t3")
            tb = Buf("clt")
            for which, (nm, sc0) in enumerate((("norm1", 8), ("norm2", 32))):
                for r in range(3):
                    op("dve", lambda e: e.scalar_tensor_tensor(out=gmv[:, l, which, :, r], in0=modv[:, l, sc0:sc0 + 8, r], scalar=1.0,
                                                               in1=vec(nm, l, 0, 8), op0=ALU.add, op1=ALU.mult), reads=[modb, vecsb], writes=[modb])
            op("act", lambda e: e.activation(out=t1[:], in_=vec("lru_lam", l, 0, 4), func=AF.Exp, scale=-1.0), reads=[vecsb], writes=[tb])
            op("dve", lambda e: e.tensor_scalar(out=t2[:], in0=t1[:], scalar1=2.0, scalar2=None, op0=ALU.add), reads=[tb], writes=[tb])
            op("dve", lambda e: e.reciprocal(out=t2[:], in_=t2[:]), reads=[tb], writes=[tb])
            op("dve", lambda e: e.tensor_tensor(out=t1[:], in0=t1[:], in1=t2[:], op=ALU.mult), reads=[tb], writes=[tb])
            op("dve", lambda e: e.tensor_tensor(out=t2[:], in0=t1[:], in1=t1[:], op=ALU.mult), reads=[tb], writes=[tb])
            op("dve", lambda e: e.tensor_scalar(out=t3[:], in0=t2[:], scalar1=0.2, scalar2=1.0 / 3.0, op0=ALU.mult, op1=ALU.add), reads=[tb], writes=[tb])
            op("dve", lambda e: e.tensor_tensor(out=t3[:], in0=t3[:], in1=t2[:], op=ALU.mult), reads=[tb], writes=[tb])
            op("dve", lambda e: e.tensor_scalar(out=t3[:], in0=t3[:], scalar1=1.0, scalar2=None, op0=ALU.add), reads=[tb], writes=[tb])
            op("dve", lambda e: e.tensor_tensor(out=t3[:], in0=t3[:], in1=t1[:], op=ALU.mult), reads=[tb], writes=[tb])
            op("dve", lambda e: e.tensor_scalar(out=clv[:, l, 0:4], in0=t3[:], scalar1=-16.0, scalar2=None, op0=ALU.mult), reads=[tb], writes=[modb])
            op("dve", lambda e: e.tensor_scalar(out=clv[:, l, 4:8], in0=t3[:], scalar1=-32.0, scalar2=None, op0=ALU.mult), reads=[tb], writes=[modb])

    modb = Buf("mod")
    S.push()
    mj0 = ModJob(0)
    mj0.start()
    for g_ in range(24):
        mj0.group(g_)
    mj0.finish()
    S.pop()

    def tap(name, src_ap, bufs):
        if name in tap_out:
            dma("pool", tap_out[name], src_ap, reads=bufs)

    if "modv" in tap_out:
        tap("modv", modv[:].rearrange("p l m r -> p (l m r)"), [modb])
        tap("gmv", gmv[:].rearrange("p l w m r -> p (l w m r)"), [modb])
        tap("clv", clv[:].rearrange("p l m -> p (l m)"), [modb])

    def rstd_from_ss(bk, bb, N, scale, rs, rsb):
        op("act", lambda e: e.activation(out=rs[:, :N], in_=bk[:, :N], func=AF.Ln, bias=epsc[:, 0:1], scale=scale), reads=[bb, constb], writes=[rsb])
        op("act", lambda e: e.activation(out=rs[:, :N], in_=rs[:, :N], func=AF.Exp, scale=-0.5), reads=[rsb], writes=[rsb])

    epsc = S.sb([128, 1], F32, "eps")
    op("pool", lambda e: e.memset(epsc[:], EPS), writes=[constb])
    onec = S.sb([128, 1], F32, "onec")
    op("pool", lambda e: e.memset(onec[:], 1.0), writes=[constb])

    def gelu(eng_mul, out_ap, in_ap, tmp, tmpb, N, rd, wr):
        p0 = in_ap.base_partition if hasattr(in_ap, "base_partition") else 0
        a = tmp[:, 0, :N]
        b_ = tmp[:, 1, :N]
        op("act", lambda e: e.activation(out=a, in_=in_ap, func=AF.Square), reads=rd, writes=[tmpb])
        op("dve", lambda e: e.tensor_scalar(out=a, in0=a, scalar1=0.044715, scalar2=1.0, op0=ALU.mult, op1=ALU.add), reads=[tmpb], writes=[tmpb])
        op("dve", lambda e: e.tensor_tensor(out=a, in0=a, in1=in_ap, op=ALU.mult), reads=[tmpb] + rd, writes=[tmpb])
        op("act", lambda e: e.activation(out=b_, in_=a, func=AF.Sigmoid, scale=GELU_C), reads=[tmpb], writes=[tmpb])
        op("dve", lambda e: e.tensor_tensor(out=out_ap, in0=b_, in1=in_ap, op=ALU.mult), reads=[tmpb] + rd, writes=wr)

    def run_rr(gens, tag=""):
        gens = list(gens)
        if DBG.get("norr") or DBG.get("norr_" + tag):
            for g_ in gens:
                for _ in g_:
                    pass
            return
        while gens:
            nxt = []
            for g_ in gens:
                try:
                    next(g_)
                    nxt.append(g_)
                except StopIteration:
                    pass
            gens = nxt

    def gelu_g(out_ap, in_ap, a, b_, tmpb, rd, wr):
        op("act", lambda e: e.activation(out=a, in_=in_ap, func=AF.Square), reads=rd, writes=[tmpb])
        yield
        op("dve", lambda e: e.tensor_scalar(out=a, in0=a, scalar1=0.044715, scalar2=1.0, op0=ALU.mult, op1=ALU.add), reads=[tmpb], writes=[tmpb])
        yield
        op("dve", lambda e: e.tensor_tensor(out=a, in0=a, in1=in_ap, op=ALU.mult), reads=[tmpb] + rd, writes=[tmpb])
        yield
        op("act", lambda e: e.activation(out=b_, in_=a, func=AF.Sigmoid, scale=GELU_C), reads=[tmpb], writes=[tmpb])
        yield
        op("dve", lambda e: e.tensor_tensor(out=out_ap, in0=b_, in1=in_ap, op=ALU.mult), reads=[tmpb] + rd, writes=wr)
        yield

    def load_x(b, ti, l, xt, xtb):
        t0, N = TILES[ti]
        src = xT_in if l == 0 else xs
        dma("sp", xt[:, :, :N], src[b, :, :, t0:t0 + N], reads=[xbuf[(b, ti)]], writes=[xtb])

    def adaln_g(l, which, r, xt, xtb, N, hT, hTb, sq, sqb, rs, rsb, tmp, tmpb):
        bk, bb = ps()
        for c in range(8):
            op("act", lambda e: e.activation(out=sq[:, c % 2, :N], in_=xt[:, c, :N], func=AF.Square), reads=[xtb], writes=[sqb[c % 2]])
            op("pe", lambda e: e.matmul(bk[:, :N], lhsT=ones[:], rhs=sq[:, c % 2, :N], start=(c == 0), stop=(c == 7)), reads=[sqb[c % 2], constb], writes=[bb])
            yield
        op("act", lambda e: e.activation(out=rs[:, :N], in_=bk[:, :N], func=AF.Ln, bias=epsc[:, 0:1], scale=1.0 / D), reads=[bb, constb], writes=[rsb])
        yield
        op("act", lambda e: e.activation(out=rs[:, :N], in_=rs[:, :N], func=AF.Exp, scale=-0.5), reads=[rsb], writes=[rsb])
        yield
        sh0 = 0 if which == 0 else 24
        for c in range(8):
            op("dve", lambda e: e.scalar_tensor_tensor(out=tmp[:, c % 2, :N], in0=xt[:, c, :N], scalar=gmv[:, l, which, c, r:r + 1],
                                                       in1=rs[:, :N], op0=ALU.mult, op1=ALU.mult), reads=[xtb, rsb, modb], writes=[tmpb[c % 2]])
            op("act", lambda e: e.activation(out=hT[:, c, :N], in_=tmp[:, c % 2, :N], func=AF.Identity, bias=modv[:, l, sh0 + c, r:r + 1]),
               reads=[tmpb[c % 2], modb], writes=[hTb])
            yield

    def adaln(l, which, r, xt, xtb, N, hT, hTb, sq, sqb, rs, rsb, tmp, tmpb):
        bk, bb = ps()
        for c in range(8):
            op("act", lambda e: e.activation(out=sq[:, c % 2, :N], in_=xt[:, c, :N], func=AF.Square), reads=[xtb], writes=[sqb[c % 2]])
            op("pe", lambda e: e.matmul(bk[:, :N], lhsT=ones[:], rhs=sq[:, c % 2, :N], start=(c == 0), stop=(c == 7)), reads=[sqb[c % 2], constb], writes=[bb])
        rstd_from_ss(bk, bb, N, 1.0 / D, rs, rsb)
        sh0 = 0 if which == 0 else 24
        for c in range(8):
            op("dve", lambda e: e.scalar_tensor_tensor(out=tmp[:, c % 2, :N], in0=xt[:, c, :N], scalar=gmv[:, l, which, c, r:r + 1],
                                                       in1=rs[:, :N], op0=ALU.mult, op1=ALU.mult), reads=[xtb, rsb, modb], writes=[tmpb[c % 2]])
            op("act", lambda e: e.activation(out=hT[:, c, :N], in_=tmp[:, c % 2, :N], func=AF.Identity, bias=modv[:, l, sh0 + c, r:r + 1]),
               reads=[tmpb[c % 2], modb], writes=[hTb])

    niter = [0]
    for b in range(2):
        for l in range(DEPTH):
            last = l == DEPTH - 1
            S.push()
            yrs = S.sb([128, 4, TOK], BF16, "yrs")
            yrsb = Buf("yrs")
            xr = S.sb([128, 2, TOK], F32, "xr")
            xrb = Buf("xr")
            ggr = S.sb([128, 2, TOK], BF16, "ggr")
            ggrb = Buf("ggr")
            S.push()
            KT = S.sb([96, 8, TOK], BF16, "KT")
            KTb = Buf("KT")
            KTrb = Buf("KTr")
            VA = S.sb([128, 18, 4, 192], BF16, "VA")
            VAb = Buf("VA")
            qnT = S.sb([128, 2, TOK], BF16, "qnT")
            qnb = Buf("qnT")
            op("pool", lambda e: e.memset(VA[:, :, :, 64:128], 1.0), writes=[VAb])

            S.push()
            xt = S.sb([128, 8, 512], F32, "xt")
            xtb = Buf("xt")
            sq = S.sb([128, 2, 512], BF16, "sq")
            sqb = [Buf("sq0"), Buf("sq1")]
            rs = S.sb([128, 512], F32, "rs")
            rsb = Buf("rs")
            tmp = S.sb([128, 2, 512], F32, "tmp")
            tmpb = [Buf("tmp0"), Buf("tmp1")]
            hT = S.sb([128, 8, 512], BF16, "hT")
            hTb = Buf("hT")
            slab = [S.sb([128, 8, 512], BF16, "slab") for _ in range(2)]
            slabb = [Buf("slab0"), Buf("slab1")]
            wkv = S.sb([128, 1024], BF16, "wkv")
            wkvb = Buf("wkv")
            sgw = S.sb([128, 4, 128], BF16, "sgw")
            sgwb = Buf("sgw")
            kvn = S.sb([128, 512], BF16, "kvn")
            kvnb = Buf("kvn")
            rq = S.sb([128, 512], F32, "rq")
            rqb = Buf("rq")
            gt = S.sb([128, 2, 512], F32, "gt")
            gtb = Buf("gt")
            rt = gt
            rtb = gtb
            ga = S.sb([128, 4, 256], F32, "ga")
            gb = S.sb([128, 4, 256], F32, "gb")
            guv = S.sb([128, 4, 512], F32, "guv")
            vgb = S.sb([128, 4, 256], BF16, "vgb")
            spn = S.sb([128, 4, 256], BF16, "spn")
            st = S.sb([128, 32], F32, "st")
            sgb = Buf("sgu")
            ropet = S.sb([128, 2, 512], F32, "ropet")
            ropeb = Buf("ropet")
            dma("sp", wkv[:], wview(l, "wkv", "(p n) -> p n", p=128), reads=[wbuf[(l, "wkv")]], writes=[wkvb])
            dma("sp", sgw[:], wview(l, "sgw", "(p g n) -> p g n", p=128, g=4), reads=[wbuf[(l, "sgw")]], writes=[sgwb])
            winv = wview(l, "win", "(p k n) -> p k n", p=128, k=8)
            nslab = [0]

            def get_slab(g):
                i = nslab[0] % 2
                nslab[0] += 1
                dma("sp", slab[i][:], winv[:, :, g * 512:(g + 1) * 512], reads=[wbuf[(l, "win")]], writes=[slabb[i]])
                return slab[i], slabb[i]

            def group1_g(ti):
                t0, N = TILES[ti]
                isctx = ti == 0
                sl, slb = get_slab(0)
                bq = [ps(), ps()]
                for mc in range(2):
                    for kc in range(8):
                        op("pe", lambda e: e.matmul(bq[mc][0][:, :N], lhsT=sl[:, kc, mc * 128:(mc + 1) * 128], rhs=hT[:, kc, :N], start=(kc == 0), stop=(kc == 7)),
                           reads=[slb, hTb], writes=[bq[mc][1]])
                        yield
                bss, bssb = ps()
                for mc in range(2):
                    op("act", lambda e: e.activation(out=sq[:, mc, :N], in_=bq[mc][0][:, :N], func=AF.Square), reads=[bq[mc][1]], writes=[sqb[mc]])
                    yield
                    op("pe", lambda e: e.matmul(bss[:, :N], lhsT=ones[:], rhs=sq[:, mc, :N], start=(mc == 0), stop=(mc == 1)), reads=[sqb[mc], constb], writes=[bssb])
                    yield
                rstd_from_ss(bss, bssb, N, 1.0 / 256, rq, rqb)
                yield
                for mc in range(2):
                    op("dve", lambda e: e.scalar_tensor_tensor(out=qnT[:, mc, t0:t0 + N], in0=bq[mc][0][:, :N], scalar=vec("qn_g", l, mc), in1=rq[:, :N],
                                                               op0=ALU.mult, op1=ALU.mult), reads=[bq[mc][1], rqb, vecsb], writes=[qnb])
                    yield
                bkv, bkvb = ps()
                for kc in range(8):
                    op("pe", lambda e: e.matmul(bkv[:, :N], lhsT=sl[:, kc, 256:384], rhs=hT[:, kc, :N], start=(kc == 0), stop=(kc == 7)), reads=[slb, hTb], writes=[bkvb])
                    yield
                bkr, bkrb = ps()
                for kc in range(8):
                    op("pe", lambda e: e.matmul(bkr[64:96, :N], lhsT=sl[:, kc, 384:416], rhs=hT[:, kc, :N], start=(kc == 0), stop=(kc == 7)), reads=[slb, hTb], writes=[bkrb])
                    yield
                if not isctx:
                    bkp, bkpb = ps()
                    for kc in range(8):
                        op("pe", lambda e: e.matmul(bkp[64:96, :N], lhsT=sl[:, kc, 416:448], rhs=hT[:, kc, :N], start=(kc == 0), stop=(kc == 7)), reads=[slb, hTb], writes=[bkpb])
                        yield
                op("act", lambda e: e.activation(out=sq[:, 0, :N], in_=bkv[:, :N], func=AF.Square), reads=[bkvb], writes=[sqb[0]])
                yield
                bss, bssb = ps()
                op("pe", lambda e: e.matmul(bss[:, :N], lhsT=ones[:], rhs=sq[:, 0, :N], start=True, stop=True), reads=[sqb[0], constb], writes=[bssb])
                yield
                rstd_from_ss(bss, bssb, N, 1.0 / 128, rq, rqb)
                yield
                op("dve", lambda e: e.scalar_tensor_tensor(out=kvn[:, :N], in0=bkv[:, :N], scalar=vec("kvn_g", l, 0), in1=rq[:, :N], op0=ALU.mult, op1=ALU.mult),
                   reads=[bkvb, rqb, vecsb], writes=[kvnb])
                yield
                if isctx:
                    op("dve", lambda e: e.tensor_copy(out=KT[64:96, :, t0:t0 + N], in_=bkr[64:96, :N].unsqueeze(1).to_broadcast([32, 8, N])), reads=[bkrb], writes=[KTrb])
                    yield
                else:
                    p0 = t0 - CTX
                    op("dve", lambda e: e.tensor_tensor(out=rt[64:96, 0, :N], in0=bkr[64:96, :N], in1=ropet[64:96, 0, :N], op=ALU.mult), reads=[bkrb, ropeb], writes=[rtb])
                    yield
                    op("dve", lambda e: e.tensor_tensor(out=rt[64:96, 1, :N], in0=bkp[64:96, :N], in1=ropet[64:96, 1, :N], op=ALU.mult), reads=[bkpb, ropeb], writes=[rtb])
                    yield
                    op("dve", lambda e: e.tensor_tensor(out=KT[64:96, :, t0:t0 + N], in0=rt[64:96, 0, :N].unsqueeze(1).to_broadcast([32, 8, N]),
                                                        in1=rt[64:96, 1, :N].unsqueeze(1).to_broadcast([32, 8, N]), op=ALU.add), reads=[rtb], writes=[KTrb])
                    yield
                for hp in range(4):
                    bk, bb = ps()
                    op("pe", lambda e: e.matmul(bk[:, :N], lhsT=wkv[:, hp * 128:(hp + 1) * 128], rhs=kvn[:, :N], start=True, stop=True), reads=[wkvb, kvnb], writes=[bb])
                    yield
                    op("act", lambda e: e.activation(out=KT[0:64, 2 * hp, t0:t0 + N], in_=bk[0:64, :N], func=AF.Copy), reads=[bb], writes=[KTb])
                    yield
                    op("act", lambda e: e.activation(out=KT[0:64, 2 * hp + 1, t0:t0 + N], in_=bk[64:128, :N], func=AF.Copy), reads=[bb], writes=[KTb])
                    yield
                for sbk in range(N // 128):
                    kt = (t0 + sbk * 128) // 128
                    bk, bb = ps()
                    op("pe", lambda e: e.matmul(bk[:, :], lhsT=kvn[:, sbk * 128:(sbk + 1) * 128], rhs=wkv[:, 512:1024], start=True, stop=True), reads=[wkvb, kvnb], writes=[bb])
                    yield
                    bk4 = bk[:, :].rearrange("p (j two d) -> p j two d", two=2, d=64)
                    op("dve", lambda e: e.tensor_copy(out=VA[:, kt, :, 0:64], in_=bk4[:, :, 0, :]), reads=[bb], writes=[VAb])
                    yield
                    op("dve", lambda e: e.tensor_copy(out=VA[:, kt, :, 128:192], in_=bk4[:, :, 1, :]), reads=[bb], writes=[VAb])
                    yield

            def G(ti):
                t0_, N_ = TILES[ti]
                r_ = 2 if ti == 0 else b
                yield from adaln_g(l, 0, r_, xt, xtb, N_, hT, hTb, sq, sqb, rs, rsb, tmp, tmpb)
                yield from group1_g(ti)

            pool_ids[:] = [0, 1, 2, 3, 4, 5, 6]
            for ti, (t0, N) in enumerate(TILES):
                isctx = ti == 0
                r = 2 if isctx else b
                if ti == 0:
                    load_x(b, ti, l, xt, xtb)
                    for _ in G(0):
                        pass
                if b == 0 and l == 0 and ti == 1:
                    cast_weights(0, ("lru", "wq", "wout", "wgu", "wo2"))
                if b == 0 and l == 0 and ti == 3:
                    cast_weights(1, ("win", "wkv", "sgw", "lru", "wq", "wout", "wgu", "wo2"))
                if ti + 1 < len(TILES):
                    t0n, Nn = TILES[ti + 1]
                    load_x(b, ti + 1, l, xt, xtb)
                    dma("sp", ropet[64:96, :, :Nn], rope_in[64:96, :, t0n - CTX:t0n - CTX + Nn], writes=[ropeb])
                sl, slb = get_slab(1)
                for mc in range(4):
                    bk, bb = ps()
                    for kc in range(8):
                        op("pe", lambda e: e.matmul(bk[:, :N], lhsT=sl[:, kc, mc * 128:(mc + 1) * 128], rhs=hT[:, kc, :N], start=(kc == 0), stop=(kc == 7)), reads=[slb, hTb], writes=[bb])
                    if mc < 2:
                        op("act", lambda e: e.activation(out=xr[:, mc, t0:t0 + N], in_=bk[:, :N], func=AF.Copy), reads=[bb], writes=[xrb])
                    else:
                        op("act", lambda e: e.activation(out=ggr[:, mc - 2, t0:t0 + N], in_=bk[:, :N], func=AF.Gelu_apprx_tanh), reads=[bb], writes=[ggrb])
                sl, slb = get_slab(2)
                nsb = N // 128
                AB = []
                for sbk in range(nsb):
                    bk, bb = ps()
                    for kc in range(8):
                        op("pe", lambda e: e.matmul(bk[:, :], lhsT=hT[:, kc, sbk * 128:(sbk + 1) * 128], rhs=sl[:, kc, :], start=(kc == 0), stop=(kc == 7)), reads=[slb, hTb], writes=[bb])
                    AB.append((bk, bb))
                for sbk in range(nsb):
                    op("act", lambda e: e.activation(out=guv[:, sbk, :], in_=AB[sbk][0][:, :], func=AF.Gelu_apprx_tanh), reads=[AB[sbk][1]], writes=[sgb])
                def sgu_tail():
                    gvv = guv[:, :nsb, 256:512]
                    guu = guv[:, :nsb, 0:256]
                    wk = ga[:, :nsb, :]
                    spt = gb[:, :nsb, :]
                    g4 = lambda ap: ap.rearrange("p s (g c) -> p s g c", g=4)
                    op("dve", lambda e: e.tensor_tensor(out=wk, in0=gvv, in1=gvv, op=ALU.mult), reads=[sgb], writes=[sgb])
                    yield
                    r16 = st[:, 0:4 * nsb].rearrange("p (s g) -> p s g", g=4)
                    op("dve", lambda e: e.tensor_reduce(out=r16, in_=g4(wk), axis=AX.X, op=ALU.add), reads=[sgb], writes=[sgb])
                    yield
                    op("act", lambda e: e.activation(out=st[:, 0:4 * nsb], in_=st[:, 0:4 * nsb], func=AF.Ln, bias=epsc[:, 0:1], scale=1.0 / 64), reads=[sgb, constb], writes=[sgb])
                    yield
                    op("act", lambda e: e.activation(out=st[:, 0:4 * nsb], in_=st[:, 0:4 * nsb], func=AF.Exp, scale=-0.5), reads=[sgb], writes=[sgb])
                    yield
                    op("dve", lambda e: e.tensor_tensor(out=g4(wk), in0=g4(gvv), in1=r16.unsqueeze(3).to_broadcast([128, nsb, 4, 64]), op=ALU.mult), reads=[sgb], writes=[sgb])
                    yield
                    op("dve", lambda e: e.tensor_tensor(out=vgb[:, :nsb, :], in0=wk, in1=sgn[:, l * 256:(l + 1) * 256].unsqueeze(1).to_broadcast([128, nsb, 256]), op=ALU.mult),
                       reads=[sgb, constb], writes=[sgb])
                    yield
                    npair = (nsb + 1) // 2
                    bias_bc = vecs[:, VOFF[("sgu_b", l)]:VOFF[("sgu_b", l)] + 4].unsqueeze(1).unsqueeze(3).to_broadcast([128, 2, 4, 64])
                    for pr in range(npair):
                        bs, bsb = ps()
                        for s2 in range(2):
                            sbk = pr * 2 + s2
                            for g in range(4):
                                op("pe", lambda e: e.matmul(bs[:, s2 * 256 + g * 64:s2 * 256 + (g + 1) * 64], lhsT=sgw[:, g, :], rhs=vgb[:, sbk, g * 64:(g + 1) * 64], start=True, stop=True),
                                   reads=[sgwb, sgb], writes=[bsb])
                                yield
                        sp2 = gb[:, pr * 2:pr * 2 + 2, :]
                        op("dve", lambda e: e.tensor_tensor(out=g4(sp2), in0=bs[:, :].rearrange("p (s g c) -> p s g c", s=2, g=4), in1=bias_bc, op=ALU.add), reads=[bsb, vecsb, sgb], writes=[sgb])
                        yield
                        op("dve", lambda e: e.tensor_tensor(out=sp2, in0=sp2, in1=guv[:, pr * 2:pr * 2 + 2, 0:256], op=ALU.mult), reads=[sgb], writes=[sgb])
                        yield
                    op("dve", lambda e: e.tensor_tensor(out=wk, in0=spt, in1=spt, op=ALU.mult), reads=[sgb], writes=[sgb])
                    yield
                    op("dve", lambda e: e.tensor_reduce(out=st[:, 16:16 + nsb], in_=wk, axis=AX.X, op=ALU.add), reads=[sgb], writes=[sgb])
                    yield
                    op("act", lambda e: e.activation(out=st[:, 16:16 + nsb], in_=st[:, 16:16 + nsb], func=AF.Ln, bias=epsc[:, 0:1], scale=1.0 / 256), reads=[sgb, constb], writes=[sgb])
                    yield
                    op("act", lambda e: e.activation(out=st[:, 16:16 + nsb], in_=st[:, 16:16 + nsb], func=AF.Exp, scale=-0.5), reads=[sgb], writes=[sgb])
                    yield
                    op("dve", lambda e: e.tensor_tensor(out=spn[:, :nsb, :], in0=spt, in1=st[:, 16:16 + nsb].unsqueeze(2).to_broadcast([128, nsb, 256]), op=ALU.mult), reads=[sgb], writes=[sgb])
                    yield
                    for sbk in range(nsb):
                        tk = t0 + sbk * 128
                        for mc in range(2):
                            op("pe", lambda e: e.transpose(tbank[:, mc * 128:(mc + 1) * 128], spn[:, sbk, mc * 128:(mc + 1) * 128], ident[:]), reads=[sgb, constb], writes=[tbankb])
                            yield
                            op("dve", lambda e: e.tensor_scalar(out=yrs[:, 2 + mc, tk:tk + 128], in0=tbank[:, mc * 128:(mc + 1) * 128], scalar1=vec("out_norm", l, 6 + mc),
                                                                scalar2=None, op0=ALU.mult), reads=[tbankb, vecsb], writes=[yrsb])
                            yield

                if ti + 1 < len(TILES):
                    t0n, Nn = TILES[ti + 1]
                    run_rr([sgu_tail(), G(ti + 1)], "p1")
                else:
                    for _ in sgu_tail():
                        pass
            pool_ids[:] = [0, 1, 2, 3, 4]
            S.pop()

            if stage >= 2:
                S.push()
                lw = S.sb([128, 2, 2, 2, 128], BF16, "lw")
                lwb = Buf("lw")
                dma("sp", lw[:], wview(l, "lru", "(p a d c n) -> p a d c n", p=128, a=2, d=2, c=2), reads=[wbuf[(l, "lru")]], writes=[lwb])
                xc2 = [S.sb([128, TOK], F32, "xc") for _ in range(2)]
                xcb2 = [Buf("xc0"), Buf("xc1")]
                xcbf2 = [S.sb([128, TOK], BF16, "xcbf") for _ in range(2)]
                xcbfb2 = [Buf("xcbf0"), Buf("xcbf1")]
                aa = S.sb([128, TOK], F32, "aa")
                aab = Buf("aa")
                bbt = S.sb([128, TOK], F32, "bbt")
                bbtb = Buf("bbt")
                hs = S.sb([128, TOK], F32, "hs")
                hsb = Buf("hs")
                rec = xr
                recb = xrb
                ltm = S.sb([128, TOK], F32, "ltm")
                ltmb = Buf("ltm")
                mj1 = None
                if b == 0 and l == 0 and DEPTH > 1:
                    mj1 = ModJob(1)
                    mj1.start()
                for c in range(2):
                    xc, xcb, xcbf, xcbfb = xc2[c], xcb2[c], xcbf2[c], xcbfb2[c]
                    cw = lambda j: vec("conv_w", l, c * 4 + j)
                    for (s0, s1) in ((0, CTX), (CTX, TOK)):
                        op("dve", lambda e: e.tensor_scalar(out=xc[:, s0:s1], in0=xr[:, c, s0:s1], scalar1=cw(2), scalar2=vec("conv_b", l, c), op0=ALU.mult, op1=ALU.add),
                           reads=[xrb, vecsb], writes=[xcb])
                        for j, sh in ((0, -2), (1, -1), (3, 1)):
                            d0, d1 = max(s0, s0 - sh), min(s1, s1 - sh)
                            op("dve", lambda e: e.scalar_tensor_tensor(out=xc[:, d0:d1], in0=xr[:, c, d0 + sh:d1 + sh], scalar=cw(j), in1=xc[:, d0:d1], op0=ALU.mult, op1=ALU.add),
                               reads=[xrb, xcb, vecsb], writes=[xcb])
                    op("act", lambda e: e.activation(out=xcbf[:, :], in_=xc[:, :], func=AF.Copy), reads=[xcb], writes=[xcbfb])
                for c in range(2):
                    xc, xcb, xcbf, xcbfb = xc2[c], xcb2[c], xcbf2[c], xcbfb2[c]
                    for d in range(2):
                        for (t0, N) in TILES:
                            br, brb = ps()
                            op("pe", lambda e: e.matmul(br[:, :N], lhsT=lw[:, 0, d, c, :], rhs=xcbf[:, t0:t0 + N], start=True, stop=True), reads=[lwb, xcbfb], writes=[brb])
                            bi, bib = ps()
                            op("pe", lambda e: e.matmul(bi[:, :N], lhsT=lw[:, 1, d, c, :], rhs=xcbf[:, t0:t0 + N], start=True, stop=True), reads=[lwb, xcbfb], writes=[bib])
                            op("act", lambda e: e.activation(out=aa[:, t0:t0 + N], in_=br[:, :N], func=AF.Sigmoid, bias=vec("lru_b_r", l, d * 2 + c)), reads=[brb, vecsb], writes=[aab])
                            op("act", lambda e: e.activation(out=bbt[:, t0:t0 + N], in_=bi[:, :N], func=AF.Sigmoid, bias=vec("lru_b_i", l, d * 2 + c)), reads=[bib, vecsb], writes=[bbtb])
                        op("act", lambda e: e.activation(out=ltm[:, :], in_=aa[:, :], func=AF.Exp, scale=clv[:, l, 4 + d * 2 + c:4 + d * 2 + c + 1]), reads=[aab, modb], writes=[ltmb])
                        op("act", lambda e: e.activation(out=aa[:, :], in_=aa[:, :], func=AF.Exp, scale=clv[:, l, d * 2 + c:d * 2 + c + 1]), reads=[aab, modb], writes=[aab])
                        op("dve", lambda e: e.tensor_tensor(out=bbt[:, :], in0=bbt[:, :], in1=xc[:, :], op=ALU.mult), reads=[bbtb, xcb], writes=[bbtb])
                        op("act", lambda e: e.activation(out=ltm[:, :], in_=ltm[:, :], func=AF.Sqrt, bias=onec[:, 0:1], scale=-1.0), reads=[ltmb, constb], writes=[ltmb])
                        op("dve", lambda e: e.tensor_tensor(out=bbt[:, :], in0=bbt[:, :], in1=ltm[:, :], op=ALU.mult), reads=[bbtb, ltmb], writes=[bbtb])
                        if mj1 is not None:
                            for g_ in range(6):
                                mj1.group((c * 2 + d) * 6 + g_)
                        if d == 0:
                            op("dve", lambda e: e.tensor_tensor_scan(out=hs[:, :], data0=aa[:, :], data1=bbt[:, :], initial=0.0, op0=ALU.mult, op1=ALU.add),
                               reads=[aab, bbtb], writes=[hsb])
                        else:
                            op("dve", lambda e: e.tensor_tensor_scan(out=bbt[:, 0:CTX][:, ::-1], data0=aa[:, 0:CTX][:, ::-1], data1=bbt[:, 0:CTX][:, ::-1], initial=0.0,
                                                                     op0=ALU.mult, op1=ALU.add), reads=[aab, bbtb], writes=[bbtb])
                            op("dve", lambda e: e.tensor_tensor_scan(out=bbt[:, CTX:TOK][:, ::-1], data0=aa[:, CTX:TOK][:, ::-1], data1=bbt[:, CTX:TOK][:, ::-1], initial=bbt[:, 0:1],
                                                                     op0=ALU.mult, op1=ALU.add), reads=[aab, bbtb], writes=[bbtb])
                            op("dve", lambda e: e.tensor_tensor(out=hs[:, :], in0=hs[:, :], in1=bbt[:, :], op=ALU.add), reads=[hsb, bbtb], writes=[hsb])
                    op("dve", lambda e: e.tensor_tensor(out=rec[:, c, :], in0=hs[:, :], in1=ggr[:, c, :], op=ALU.mult), reads=[hsb, ggrb], writes=[recb])
                if "rec" in tap_out and b == 0 and l == 0:
                    tap("rec", rec[:].rearrange("p c t -> p (c t)"), [recb])
                if mj1 is not None:
                    mj1.finish()
                lsq = bbt[:, 0:512].bitcast(BF16).rearrange("p (c t) -> p c t", c=2)
                lsqb = [bbtb, bbtb]
                lrs = ltm
                lrsb = ltmb
                for (t0, N) in TILES:
                    bss, bssb = ps()
                    for c in range(2):
                        op("act", lambda e: e.activation(out=lsq[:, c, :N], in_=rec[:, c, t0:t0 + N], func=AF.Square), reads=[recb], writes=[lsqb[c]])
                        op("pe", lambda e: e.matmul(bss[:, :N], lhsT=ones[:], rhs=lsq[:, c, :N], start=(c == 0), stop=(c == 1)), reads=[lsqb[c], constb], writes=[bssb])
                    rstd_from_ss(bss, bssb, N, 1.0 / 256, lrs, lrsb)
                    for c in range(2):
                        op("dve", lambda e: e.scalar_tensor_tensor(out=yrs[:, c, t0:t0 + N], in0=rec[:, c, t0:t0 + N], scalar=vec("out_norm", l, 4 + c), in1=lrs[:, :N],
                                                                   op0=ALU.mult, op1=ALU.mult), reads=[recb, lrsb, vecsb], writes=[yrsb])
                S.pop()

            if b == 0 and l == 0:
                S.push()
                stg = S.sb([128, TOK], F32, "stg")
                stgb = Buf("stg")
                for nm, src, sb_, P_, C_, T_ in (("qnT", qnT, qnb, 128, 2, TOK), ("KT", KT, KTb, 96, 8, TOK), ("yrs", yrs, yrsb, 128, 4, TOK)):
                    if nm in tap_out:
                        for c_ in range(C_):
                            op("dve", lambda e: e.tensor_copy(out=stg[0:P_, 0:T_], in_=src[0:P_, c_, :]), reads=[sb_], writes=[stgb])
                            dma("pool", tap_out[nm][:, c_ * T_:(c_ + 1) * T_], stg[0:P_, 0:T_], reads=[stgb])
                S.pop()
                if "xr" in tap_out:
                    tap("xr", xr[:].rearrange("p c t -> p (c t)"), [xrb])
                    tap("ggr", ggr[:].rearrange("p c t -> p (c t)"), [ggrb])

            ya = None
            if stage >= 3:
                S.push()
                ya = xr[:].rearrange("p c t -> p (c t)").bitcast(BF16).rearrange("p (c t) -> p c t", c=4)
                yab = xrb
                wq = S.sb([128, 2, 8, 128], BF16, "wq")
                wqb_ = Buf("wq")
                dma("sp", wq[:], wview(l, "wq", "(p k h n) -> p k h n", p=128, k=2, h=8), reads=[wbuf[(l, "wq")]], writes=[wqb_])
                QT2 = [S.sb([96, 8, 512], BF16, "QT") for _ in range(2)]
                QTb2 = [Buf("QT0"), Buf("QT1")]
                qrt = S.sb([96, 2, 512], F32, "qrt")
                qrtb = Buf("qrt")
                PT = [S.sb([128, 2, 512], BF16, "PT") for _ in range(3)]
                PTb = [Buf("PT%d" % i) for i in range(3)]
                pool_ids[:] = [4, 7]
                ropeq = S.sb([128, 2, 512], F32, "ropeq")
                ropeqb = Buf("ropeq")
                Rr = S.sb([128, 512], F32, "Rr")
                Rrb = Buf("Rr")
                att = S.sb([128, 4, 512], F32, "att")
                attb = Buf("att")
                asq = S.sb([128, 2, 512], BF16, "asq")
                asqb = [Buf("asq0"), Buf("asq1")]
                ars = S.sb([128, 512], F32, "ars")
                arsb = Buf("ars")
                scale = 1.0 / float(np.sqrt(96.0))
                tiles_a = [ti for ti in range(len(TILES)) if not (ti == 0 and last)]
                npt_ = [0]

                def q_proj(ti, heads=range(8)):
                    t0, N = TILES[ti]
                    isctx = ti == 0
                    QT, QTb = QT2[ti % 2], QTb2[ti % 2]
                    if not isctx and 0 in heads:
                        dma("sp", ropeq[64:96, :, :N], rope_in[64:96, :, t0 - CTX:t0 - CTX + N], writes=[ropeqb])
                    for h in heads:
                        ba, bab = ps()
                        for kc in range(2):
                            op("pe", lambda e: e.matmul(ba[0:96, :N], lhsT=wq[:, kc, h, 0:96], rhs=qnT[:, kc, t0:t0 + N], start=(kc == 0), stop=(kc == 1)), reads=[wqb_, qnb], writes=[bab])
                        op("act", lambda e: e.activation(out=QT[0:64, h, :N], in_=ba[0:64, :N], func=AF.Copy), reads=[bab], writes=[QTb])
                        if isctx:
                            op("act", lambda e: e.activation(out=QT[64:96, h, :N], in_=ba[64:96, :N], func=AF.Copy), reads=[bab], writes=[QTb])
                        else:
                            bp, bpb = ps()
                            for kc in range(2):
                                op("pe", lambda e: e.matmul(bp[64:96, :N], lhsT=wq[:, kc, h, 96:128], rhs=qnT[:, kc, t0:t0 + N], start=(kc == 0), stop=(kc == 1)), reads=[wqb_, qnb], writes=[bpb])
                            op("dve", lambda e: e.tensor_tensor(out=qrt[64:96, 0, :N], in0=ba[64:96, :N], in1=ropeq[64:96, 0, :N], op=ALU.mult), reads=[bab, ropeqb], writes=[qrtb])
                            op("dve", lambda e: e.tensor_tensor(out=qrt[64:96, 1, :N], in0=bp[64:96, :N], in1=ropeq[64:96, 1, :N], op=ALU.mult), reads=[bpb, ropeqb], writes=[qrtb])
                            op("pool", lambda e: e.tensor_tensor(out=QT[64:96, h, :N], in0=qrt[64:96, 0, :N], in1=qrt[64:96, 1, :N], op=ALU.add), reads=[qrtb], writes=[QTb])

                def emit_s(ti, h, kp):
                    t0, N = TILES[ti]
                    QT, QTb = QT2[ti % 2], QTb2[ti % 2]
                    bi = npt_[0] % 2
                    big, bgb = bigs[bi], bigb[bi]
                    for j in range(2):
                        kt = 2 * kp + j
                        op("pe", lambda e: e.matmul(big[:, j * 512:j * 512 + N], lhsT=KT[0:96, h, kt * 128:(kt + 1) * 128], rhs=QT[0:96, h, :N], start=True, stop=True),
                           reads=[KTb, KTrb, QTb], writes=[bgb])
                    pi = npt_[0] % 3
                    npt_[0] += 1
                    op("act", lambda e: e.activation(out=PT[pi][:, :, :N], in_=big[:, :].rearrange("p (j n) -> p j n", j=2)[:, :, :N], func=AF.Exp, scale=scale), reads=[bgb], writes=[PTb[pi]])
                    return pi

                def att_norm(ti):
                    t0, N = TILES[ti]
                    if "att" in tap_out and b == 0 and l == 0 and ti == 1:
                        tap("att", att[:].rearrange("p c t -> p (c t)"), [attb])
                    bss, bssb = ps()
                    for c in range(4):
                        op("act", lambda e: e.activation(out=asq[:, c % 2, :N], in_=att[:, c, :N], func=AF.Square), reads=[attb], writes=[asqb[c % 2]])
                        op("pe", lambda e: e.matmul(bss[:, :N], lhsT=ones[:], rhs=asq[:, c % 2, :N], start=(c == 0), stop=(c == 3)), reads=[asqb[c % 2], constb], writes=[bssb])
                    rstd_from_ss(bss, bssb, N, 1.0 / 512, ars, arsb)
                    for c in range(4):
                        op("dve", lambda e: e.scalar_tensor_tensor(out=ya[:, c, t0:t0 + N], in0=att[:, c, :N], scalar=vec("out_norm", l, c), in1=ars[:, :N],
                                                                   op0=ALU.mult, op1=ALU.mult), reads=[attb, arsb, vecsb], writes=[yab])

                def emit_pv(ti, h, kp, pi):
                    t0, N = TILES[ti]
                    nkt = 2 if ti == 0 else 18
                    ou, oub = banks[5 + h % 2], bankb[5 + h % 2]
                    for j in range(2):
                        kt = 2 * kp + j
                        lhs = VA[:, kt, h // 2, 0:128] if h % 2 == 0 else VA[:, kt, h // 2, 64:192]
                        op("pe", lambda e: e.matmul(ou[:, :N], lhsT=lhs, rhs=PT[pi][:, j, :N], start=(kt == 0), stop=(kt == nkt - 1)), reads=[VAb, PTb[pi]], writes=[oub])
                    if 2 * kp + 1 == nkt - 1:
                        po = (h % 2) * 64
                        pd = 64 - po
                        op("dve", lambda e: e.reciprocal(out=Rr[po:po + 64, :N], in_=ou[pd:pd + 64, :N]), reads=[oub], writes=[Rrb])
                        op("dve", lambda e: e.tensor_tensor(out=att[po:po + 64, h // 2, :N], in0=ou[po:po + 64, :N], in1=Rr[po:po + 64, :N], op=ALU.mult), reads=[oub, Rrb], writes=[attb])
                        if h == 7:
                            att_norm(ti)

                items = []
                for k_, ti in enumerate(tiles_a):
                    nkt = 2 if ti == 0 else 18
                    for h in range(8):
                        for kp in range(nkt // 2):
                            items.append((ti, h, kp, k_))
                LOOK = DBG.get("look", 2)
                pend = []
                q_proj(tiles_a[0])
                for (ti, h, kt, k_) in items:
                    if kt == 0 and k_ + 1 < len(tiles_a):
                        q_proj(tiles_a[k_ + 1], [h])
                    pend.append((ti, h, kt, emit_s(ti, h, kt)))
                    if len(pend) > LOOK:
                        emit_pv(*pend.pop(0))
                while pend:
                    emit_pv(*pend.pop(0))
                pool_ids[:] = [0, 1, 2, 3, 4]
                S.pop()
            S.pop()
            if stage >= 4:
                S.push()
                xt = S.sb([128, 8, 512], F32, "xt2")
                xtb = Buf("xt2")
                sq = S.sb([128, 2, 512], BF16, "sq2")
                sqb = [Buf("sq0"), Buf("sq1")]
                rs = S.sb([128, 512], F32, "rs2")
                rsb = Buf("rs2")
                tmp = S.sb([128, 2, 512], F32, "tmp2")
                tmpb = [Buf("tmp0"), Buf("tmp1")]
                hT = S.sb([128, 8, 512], BF16, "hT2")
                hTb = Buf("hT2")
                h1 = S.sb([128, NJ, 512], BF16, "h1")
                h1b = Buf("h1")
                NWG, NW2 = 3, 2
                woa = S.sb([128, 8, 8, 128], BF16, "woa")
                woab = Buf("woa")
                wg = [S.sb([128, 2, 8, 256], BF16, "wg") for _ in range(NWG)]
                wgb = [Buf("wg%d" % i) for i in range(NWG)]
                w2 = [S.sb([128, 2, NJ, 128], BF16, "w2") for _ in range(NW2)]
                w2b = [Buf("w2%d" % i) for i in range(NW2)]
                sgs = S.sb([128, 2, 512], F32, "sgs")
                sgsb = [Buf("sgs0"), Buf("sgs1")]
                woutv = wview(l, "wout", "(m p k n) -> m p k n", m=8, p=128, k=8)
                wguv = wview(l, "wgu", "(j p k n) -> j p k n", j=NJ, p=128, k=8)
                wo2v = wview(l, "wo2", "(m p j n) -> m p j n", m=8, p=128, j=NJ)
                xt_2 = [xt, S.sb([128, 8, 512], F32, "xt3")]
                xtb_2 = [xtb, Buf("xt3")]
                cnts = dict(nwg=0, nw2=0, nsg=0)
                tiles_f = [ti for ti in range(len(TILES)) if not (ti == 0 and last)]

                def f_wout(ti):
                    t0, N = TILES[ti]
                    r = 2 if ti == 0 else b
                    xt, xtb = xt_2[ti % 2], xtb_2[ti % 2]
                    load_x(b, ti, l, xt, xtb)
                    dma("sp", woa[:], woutv.rearrange("m p k n -> p m k n"), reads=[wbuf[(l, "wout")]], writes=[woab])
                    for m in range(8):
                        bk, bb = ps()
                        for kc in range(8):
                            rhs = ya[:, kc, t0:t0 + N] if kc < 4 else yrs[:, kc - 4, t0:t0 + N]
                            op("pe", lambda e: e.matmul(bk[:, :N], lhsT=woa[:, m, kc, :], rhs=rhs, start=(kc == 0), stop=(kc == 7)), reads=[woab, yab, yrsb], writes=[bb])
                        op("dve", lambda e: e.scalar_tensor_tensor(out=xt[:, m, :N], in0=bk[:, :N], scalar=modv[:, l, 16 + m, r:r + 1], in1=xt[:, m, :N], op0=ALU.mult, op1=ALU.add),
                           reads=[bb, modb, xtb], writes=[xtb])
                    if "xmid" in tap_out and b == 0 and l == 0 and ti == 1:
                        tap("xmid", xt[:].rearrange("p c t -> p (c t)"), [xtb])

                def f_norm(ti):
                    t0, N = TILES[ti]
                    r = 2 if ti == 0 else b
                    adaln(l, 1, r, xt_2[ti % 2], xtb_2[ti % 2], N, hT, hTb, sq, sqb, rs, rsb, tmp, tmpb)

                def f_p1(ti):
                    t0, N = TILES[ti]
                    for j in range(NJ):
                        if j % 2 == 0:
                            wi = cnts["nwg"] % NWG
                            cnts["nwg"] += 1
                            dma("sp", wg[wi][:], wguv[j:j + 2].rearrange("j p k n -> p j k n"), reads=[wbuf[(l, "wgu")]], writes=[wgb[wi]])
                        jj = j % 2
                        bg, bgb = ps()
                        for kc in range(8):
                            op("pe", lambda e: e.matmul(bg[:, :N], lhsT=wg[wi][:, jj, kc, 0:128], rhs=hT[:, kc, :N], start=(kc == 0), stop=(kc == 7)), reads=[wgb[wi], hTb], writes=[bgb])
                        bu, bub = ps()
                        for kc in range(8):
                            op("pe", lambda e: e.matmul(bu[:, :N], lhsT=wg[wi][:, jj, kc, 128:256], rhs=hT[:, kc, :N], start=(kc == 0), stop=(kc == 7)), reads=[wgb[wi], hTb], writes=[bub])
                        si = cnts["nsg"] % 2
                        cnts["nsg"] += 1
                        op("act", lambda e: e.activation(out=sgs[:, si, :N], in_=bg[:, :N], func=AF.Silu), reads=[bgb], writes=[sgsb[si]])
                        op("dve", lambda e: e.tensor_tensor(out=h1[:, j, :N], in0=sgs[:, si, :N], in1=bu[:, :N], op=ALU.mult), reads=[sgsb[si], bub], writes=[h1b])

                def f_p2(ti, ms):
                    t0, N = TILES[ti]
                    r = 2 if ti == 0 else b
                    xt, xtb = xt_2[ti % 2], xtb_2[ti % 2]
                    for m in ms:
                        if m % 2 == 0:
                            cnts["w2i"] = cnts["nw2"] % NW2
                            cnts["nw2"] += 1
                            dma("sp", w2[cnts["w2i"]][:], wo2v[m:m + 2].rearrange("m p j n -> p m j n"), reads=[wbuf[(l, "wo2")]], writes=[w2b[cnts["w2i"]]])
                        wi = cnts["w2i"]
                        bk, bb = ps()
                        for j in range(NJ):
                            op("pe", lambda e: e.matmul(bk[:, :N], lhsT=w2[wi][:, m % 2, j, :], rhs=h1[:, j, :N], start=(j == 0), stop=(j == NJ - 1)), reads=[w2b[wi], h1b], writes=[bb])
                        op("dve", lambda e: e.scalar_tensor_tensor(out=xt[:, m, :N], in0=bk[:, :N], scalar=modv[:, l, 40 + m, r:r + 1], in1=xt[:, m, :N], op0=ALU.mult, op1=ALU.add),
                           reads=[bb, modb, xtb], writes=[xtb])

                def f_fin(ti):
                    t0, N = TILES[ti]
                    xt, xtb = xt_2[ti % 2], xtb_2[ti % 2]
                    if not last:
                        dma("act", xs[b, :, :, t0:t0 + N], xt[:, :, :N], reads=[xtb], writes=[xbuf[(b, ti)]])
                    else:
                        bss, bssb = ps()
                        for c in range(8):
                            op("act", lambda e: e.activation(out=sq[:, c % 2, :N], in_=xt[:, c, :N], func=AF.Square), reads=[xtb], writes=[sqb[c % 2]])
                            op("pe", lambda e: e.matmul(bss[:, :N], lhsT=ones[:], rhs=sq[:, c % 2, :N], start=(c == 0), stop=(c == 7)), reads=[sqb[c % 2], constb], writes=[bssb])
                        rstd_from_ss(bss, bssb, N, 1.0 / D, rs, rsb)
                        for c in range(8):
                            op("dve", lambda e: e.scalar_tensor_tensor(out=xt[:, c, :N], in0=xt[:, c, :N], scalar=vec("final", None, c), in1=rs[:, :N], op0=ALU.mult, op1=ALU.mult),
                               reads=[xtb, rsb, vecsb], writes=[xtb])
                        dma("act", outT[b, :, :, t0 - CTX:t0 - CTX + N], xt[:, :, :N], reads=[xtb], writes=[outbuf])

                f_wout(tiles_f[0])
                if stage >= 5:
                    f_norm(tiles_f[0])
                for k_, ti in enumerate(tiles_f):
                    nxt = tiles_f[k_ + 1] if k_ + 1 < len(tiles_f) else None
                    if stage >= 5:
                        f_p1(ti)
                    if nxt is not None:
                        f_wout(nxt)
                    if stage >= 5:
                        f_p2(ti, range(0, 4))
                        if nxt is not None:
                            f_norm(nxt)
                        f_p2(ti, range(4, 8))
                    f_fin(ti)
                S.pop()
            S.pop()
            niter[0] += 1
            if niter[0] >= DBG.get("iters", 99):
                break
        if niter[0] >= DBG.get("iters", 99):
            break
            if stage < 99 and not (stage >= 5):
                break
        if stage < 99:
            break
    S.finish()
    return nc, S


def _prep_inputs(inp):
    wl = [_layer_weights(inp, l) for l in range(DEPTH)]
    wada = np.concatenate([np.ascontiguousarray(inp["w_ada"][l].reshape(8, 128, 6144).transpose(1, 0, 2)).reshape(ADA_ROWS, 2048) for l in range(DEPTH)], 0)
    sgn = np.ascontiguousarray(np.broadcast_to(np.concatenate([inp["sgu_norm"][l] for l in range(DEPTH)])[None, :], (128, DEPTH * 256))).astype(np.float32)
    rope = _rope_tables()
    ident = np.eye(128, dtype=np.float32)
    maps = []
    for core in range(NCORES):
        bs = [2 * core, 2 * core + 1]
        xfull = np.concatenate([inp["ctx"][bs], inp["x"][bs]], axis=1)
        xT = np.ascontiguousarray(xfull.reshape(2, TOK, 8, 128).transpose(0, 3, 2, 1))
        vecs = _vecs(inp, [inp["c"][bs[0]], inp["c"][bs[1]], inp["c_ctx"]])
        maps.append({"xT": xT, "wl0": wl[0], "wl1": wl[1], "wada": wada, "vecs": vecs, "sgn": sgn, "rope": rope, "ident": ident})
    return maps


def kernel(**inputs):
    inp = {k: np.asarray(v, dtype=np.float32) for k, v in inputs.items()}
    _, S0 = build_program()
    nc, _ = build_program(marks=S0.used)
    maps = _prep_inputs(inp)
    res = run_bass_kernel_spmd(nc, maps, core_ids=list(range(NCORES)))
    out = np.empty((16, SEQ, D), np.float32)
    for core in range(NCORES):
        oT = res.results[core]["outT"]
        out[2 * core:2 * core + 2] = oT.transpose(0, 3, 2, 1).reshape(2, SEQ, D)
    return out
```

```python
import numpy as np
from contextlib import ExitStack
import concourse.bass as bass
import concourse.mybir as mybir
from concourse.bass_utils import run_bass_kernel_spmd

F32 = mybir.dt.float32
BF16 = mybir.dt.bfloat16
AF = mybir.ActivationFunctionType
ALU = mybir.AluOpType
AX = mybir.AxisListType

NCORES = 8
D = 1024
SEQ = 2048
CTX = 256
TOK = SEQ + CTX
DEPTH = 2
DFF = 2816
NJ = DFF // 128
EPS = 1e-6
TILES = [(0, 256), (256, 512), (768, 512), (1280, 512), (1792, 512)]
NDMA = 40
GELU_C = 1.5957691216057308

WNAMES = [("win", 768), ("wq", 128), ("wkv", 64), ("wout", 512), ("wgu", 2816), ("wo2", 1408), ("lru", 64), ("sgw", 32)]
WROWS = sum(r for _, r in WNAMES)
WOFF = {}
_o = 0
for _n, _r in WNAMES:
    WOFF[_n] = (_o, _r)
    _o += _r
ADA_ROWS = 3072

VEC_FIELDS = [("norm1", 8), ("norm2", 8), ("b_ada", 48), ("qn_g", 2), ("kvn_g", 1), ("conv_w", 8), ("conv_b", 2),
              ("lru_b_r", 4), ("lru_b_i", 4), ("lru_lam", 4), ("out_norm", 8), ("sgu_b", 4)]
VOFF = {}
_o = 0
for _l in range(DEPTH):
    for _n, _w in VEC_FIELDS:
        VOFF[(_n, _l)] = _o
        _o += _w
VOFF["final"] = _o
_o += 8
VOFF["cT"] = _o
_o += 24
NV = _o

ROPE_PERM = np.array(list(range(8, 16)) + list(range(0, 8)) + list(range(24, 32)) + list(range(16, 24)))
ROPE_SIGN = np.array([-1.0] * 8 + [1.0] * 8 + [-1.0] * 8 + [1.0] * 8, np.float32)


def _pc(v, nchunk):
    return np.ascontiguousarray(v.reshape(nchunk, 128).T)


def _rope_tables():
    rows = SEQ // 64
    row = np.repeat(np.arange(rows, dtype=np.float32), 64)
    col = np.tile(np.arange(64, dtype=np.float32), rows)
    half = 16
    freq = (np.float32(10000.0) ** (-np.arange(0, half, 2, dtype=np.float32) / np.float32(half))).astype(np.float32)
    ar = row[:, None] * freq
    ac = col[:, None] * freq
    ang = np.concatenate([ar, ar, ac, ac], axis=-1).astype(np.float32)
    cos = np.cos(ang).astype(np.float32)
    sin = np.sin(ang).astype(np.float32) * ROPE_SIGN[None, :]
    t = np.zeros((128, 2, SEQ), np.float32)
    t[64:96, 0, :] = cos.T
    t[64:96, 1, :] = sin.T
    return t


def _layer_weights(inp, l):
    w_in = inp["w_in"][l]
    kr = w_in[:, 384:416]
    g1 = np.concatenate([w_in[:, 0:256], w_in[:, 256:384], kr, kr[:, ROPE_PERM], np.zeros((D, 64), np.float32)], 1)
    g2 = w_in[:, 416:928]
    g3 = w_in[:, 928:1440]
    win = np.concatenate([g1, g2, g3], 1).reshape(8, 128, 1536).transpose(1, 0, 2)
    wqb = inp["w_q_b"][l].reshape(256, 8, 96)
    wq = np.concatenate([wqb, wqb[:, :, 64 + ROPE_PERM]], 2).reshape(2, 128, 8, 128).transpose(1, 0, 2, 3)
    wkvb = inp["w_kv_b"][l].reshape(128, 8, 128)
    wkv = np.concatenate([wkvb[:, :, :64].reshape(128, 512), wkvb[:, :, 64:].reshape(128, 512)], 1)
    wout = inp["w_out"][l].reshape(8, 128, 8, 128).transpose(2, 1, 0, 3)
    wfi = inp["w_ffn_in"][l].reshape(8, 128, 2, NJ, 128)
    wgu = wfi.transpose(3, 1, 0, 2, 4)
    wo2 = inp["w_ffn_out"][l].reshape(NJ, 128, 8, 128).transpose(2, 1, 0, 3)
    lru = np.zeros((128, 2, 2, 2, 128), np.float32)
    for ri, nm in enumerate(("lru_w_r", "lru_w_i")):
        w = inp[nm][l]
        for d in range(2):
            for c in range(2):
                for hh in range(2):
                    lru[hh * 64:(hh + 1) * 64, ri, d, c, hh * 64:(hh + 1) * 64] = w[d, 2 * c + hh]
    sgw = inp["sgu_w"][l].transpose(2, 0, 1)
    parts = [win, wq, wkv, wout, wgu, wo2, lru, sgw]
    flat = np.concatenate([np.ascontiguousarray(p, dtype=np.float32).reshape(-1) for p in parts])
    assert flat.size == WROWS * 2048
    return flat.reshape(WROWS, 2048)


def _vecs(inp, cvecs):
    v = np.zeros((128, NV), np.float32)
    for l in range(DEPTH):
        def put(name, arr):
            o = VOFF[(name, l)]
            v[:, o:o + arr.shape[1]] = arr
        put("norm1", _pc(inp["norm1"][l], 8))
        put("norm2", _pc(inp["norm2"][l], 8))
        put("b_ada", _pc(inp["b_ada"][l], 48))
        put("qn_g", _pc(inp["q_a_norm"][l], 2))
        put("kvn_g", _pc(inp["kv_a_norm"][l], 1))
        cw = inp["conv_w"][l]
        put("conv_w", np.ascontiguousarray(cw.reshape(4, 2, 128).transpose(2, 1, 0)).reshape(128, 8))
        put("conv_b", _pc(inp["conv_b"][l], 2))
        for nm, key in (("lru_b_r", "lru_b_r"), ("lru_b_i", "lru_b_i"), ("lru_lam", "lru_lam")):
            a = inp[key][l]
            put(nm, np.ascontiguousarray(a.reshape(2, 2, 128).transpose(2, 0, 1)).reshape(128, 4))
        put("out_norm", _pc(inp["out_norm"][l], 8))
        put("sgu_b", np.ascontiguousarray(inp["sgu_b"][l].T))
    v[:, VOFF["final"]:VOFF["final"] + 8] = _pc(inp["final_norm"], 8)
    cT = np.stack([_pc(cv, 8) for cv in cvecs], axis=2)
    v[:, VOFF["cT"]:VOFF["cT"] + 24] = cT.reshape(128, 24)
    return v


ENG = ["pe", "act", "dve", "pool", "sp"]
DBG = {"norr_sgu": 1}


class Buf:
    __slots__ = ("w", "r", "name")

    def __init__(self, name=""):
        self.w = None
        self.r = {}
        self.name = name


class Sched:
    def __init__(self, nc, marks=None):
        self.nc = nc
        self.marks = None if marks is None else {e: {idx: i + 1 for i, idx in enumerate(sorted(marks[e]))} for e in marks}
        self.used = {e: set() for e in ENG}
        self.eng = {"pe": nc.tensor, "act": nc.scalar, "dve": nc.vector, "pool": nc.gpsimd, "sp": nc.sync}
        self.cnt = {e: 0 for e in ENG}
        self.known = {e: {} for e in ENG}
        self.root = ExitStack()
        self.esem = {e: self.root.enter_context(nc.semaphore("s_" + e)) for e in ENG}
        self.dsem = [self.root.enter_context(nc.semaphore("d%d" % i)) for i in range(NDMA)]
        self.dcnt = [0] * NDMA
        self.dbar = [0] * NDMA
        self.dnext = 0
        self.scopes = [self.root]
        self.nid = 0
        self.ninstr = 0

    def sb(self, shape, dtype, name=None):
        self.nid += 1
        return self.scopes[-1].enter_context(self.nc.sbuf_tensor("%s_%d" % (name or "t", self.nid), list(shape), dtype))

    def psum(self, shape, dtype, name=None):
        self.nid += 1
        return self.root.enter_context(self.nc.psum_tensor("%s_%d" % (name or "ps", self.nid), list(shape), dtype))

    def push(self):
        st = ExitStack()
        self.scopes.append(st)

    def pop(self):
        self.barrier()
        self.scopes.pop().close()

    def _sem(self, k):
        return self.esem[k] if isinstance(k, str) else self.dsem[k[1]]

    def _waits(self, e, needs):
        kn = self.known[e]
        for k, v in needs.items():
            if kn.get(k, 0) >= v:
                continue
            kn[k] = v
            val = v
            if isinstance(k, str):
                self.used[k].add(v)
                if self.marks is not None:
                    val = self.marks[k][v]
            self.eng[e].wait_ge(self._sem(k), val)
            self.ninstr += 1

    def op(self, e, fn, reads=(), writes=()):
        needs = {}
        for b in reads:
            if b.w is not None and needs.get(b.w[0], 0) < b.w[1]:
                needs[b.w[0]] = b.w[1]
        for b in writes:
            if b.w is not None and b.w[0] != e and needs.get(b.w[0], 0) < b.w[1]:
                needs[b.w[0]] = b.w[1]
            for k, v in b.r.items():
                if k != e and needs.get(k, 0) < v:
                    needs[k] = v
        self._waits(e, needs)
        ins = fn(self.eng[e])
        self.cnt[e] += 1
        v = self.cnt[e]
        if self.marks is None or v in self.marks[e]:
            ins.then_inc(self.esem[e], 1)
        self.ninstr += 1
        for b in reads:
            b.r[e] = v
        for b in writes:
            b.w = (e, v)
            b.r = {}
        return ins

    def dma(self, q, out, in_, reads=(), writes=(), nobar=False, **kw):
        slot = self.dnext
        self.dnext = (self.dnext + 1) % NDMA
        key = ("d", slot)
        needs = {}
        if self.dcnt[slot]:
            needs[key] = 16 * self.dcnt[slot]
        for b in reads:
            if b.w is not None and needs.get(b.w[0], 0) < b.w[1]:
                needs[b.w[0]] = b.w[1]
        for b in writes:
            if b.w is not None and needs.get(b.w[0], 0) < b.w[1]:
                needs[b.w[0]] = b.w[1]
            for k, v in b.r.items():
                if needs.get(k, 0) < v:
                    needs[k] = v
        self._waits(q, needs)
        self.eng[q].dma_start(out=out, in_=in_, **kw).then_inc(self.dsem[slot], 16)
        self.ninstr += 1
        self.dcnt[slot] += 1
        v = 16 * self.dcnt[slot]
        if not nobar:
            self.dbar[slot] = self.dcnt[slot]
        for b in reads:
            b.r[key] = v
        for b in writes:
            b.w = (key, v)
            b.r = {}

    def barrier(self, final=False):
        for e in ENG:
            needs = {k: self.cnt[k] for k in ENG if k != e and self.cnt[k] > 0}
            for s in range(NDMA):
                n_ = self.dcnt[s] if final else self.dbar[s]
                if n_:
                    needs[("d", s)] = 16 * n_
            self._waits(e, needs)

    def finish(self):
        self.barrier(final=True)
        self.root.close()


def build_program(stage=99, taps=(), marks=None):
    nc = bass.Bass("TRN2", target_bir_lowering=False)
    S = Sched(nc, marks)
    op, dma = S.op, S.dma

    xT_in = nc.dram_tensor("xT", [2, 128, 8, TOK], F32, kind="ExternalInput").ap()
    wl_in = [nc.dram_tensor("wl%d" % l, [WROWS, 2048], F32, kind="ExternalInput").ap() for l in range(DEPTH)]
    wada_in = nc.dram_tensor("wada", [DEPTH * ADA_ROWS, 2048], F32, kind="ExternalInput").ap()
    vecs_in = nc.dram_tensor("vecs", [128, NV], F32, kind="ExternalInput").ap()
    sgn_in = nc.dram_tensor("sgn", [128, DEPTH * 256], F32, kind="ExternalInput").ap()
    rope_in = nc.dram_tensor("rope", [128, 2, SEQ], F32, kind="ExternalInput").ap()
    ident_in = nc.dram_tensor("ident", [128, 128], F32, kind="ExternalInput").ap()
    outT = nc.dram_tensor("outT", [2, 128, 8, SEQ], F32, kind="ExternalOutput").ap()
    wb = [nc.dram_tensor("wb%d" % l, [WROWS, 2048], BF16, kind="Internal").ap() for l in range(DEPTH)]
    xs = nc.dram_tensor("xs", [2, 128, 8, TOK], F32, kind="Internal").ap()
    tap_out = {}
    for name, shape in taps:
        tap_out[name] = nc.dram_tensor("tap_" + name, list(shape), F32, kind="ExternalOutput").ap()

    wbuf = {(l, n): Buf("wb%d_%s" % (l, n)) for l in range(DEPTH) for n, _ in WNAMES}
    adabuf = [Buf("ada%d" % l) for l in range(DEPTH)]
    xbuf = {(b, ti): Buf("x%d_%d" % (b, ti)) for b in range(2) for ti in range(len(TILES))}
    outbuf = Buf("out")

    def wview(l, name, pattern, **kw):
        o, r = WOFF[name]
        return wb[l][o:o + r, :].rearrange("r c -> (r c)").rearrange(pattern, **kw)

    bigs = [S.psum([128, 1024], F32, "big") for _ in range(2)]
    bigb = [Buf("big0"), Buf("big1")]
    banks = [bigs[0][:, 0:512], bigs[0][:, 512:1024], bigs[1][:, 0:512], bigs[1][:, 512:1024]]
    banks += [S.psum([128, 512], F32, "bank") for _ in range(3)]
    bankb = [Buf("bank%d" % i) for i in range(7)]
    tbank = S.psum([128, 1024], BF16, "tbank")
    tbankb = Buf("tbank")
    banks.append(tbank[:].bitcast(F32))
    bankb.append(tbankb)
    pool_ids = [0, 1, 2, 3, 4]
    rr = [0]

    def ps():
        i = pool_ids[rr[0] % len(pool_ids)]
        rr[0] += 1
        return banks[i], bankb[i]

    vecs = S.sb([128, NV], F32, "vecs")
    vecsb = Buf("vecs")
    sgn = S.sb([128, DEPTH * 256], F32, "sgn")
    ident_f = S.sb([128, 128], F32, "identf")
    ident = S.sb([128, 128], BF16, "ident")
    ones = S.sb([128, 128], BF16, "ones")
    modv = S.sb([128, DEPTH, 48, 3], F32, "modv")
    gmv = S.sb([128, DEPTH, 2, 8, 3], F32, "gmv")
    clv = S.sb([128, DEPTH, 8], F32, "clv")
    constb = Buf("const")
    dma("sp", vecs[:], vecs_in[:, :], writes=[vecsb])
    dma("sp", sgn[:], sgn_in[:, :], writes=[constb])
    dma("sp", ident_f[:], ident_in[:, :], writes=[constb])
    op("dve", lambda e: e.tensor_copy(out=ident[:], in_=ident_f[:]), reads=[constb], writes=[constb])
    op("pool", lambda e: e.memset(ones[:], 1.0), writes=[constb])

    def vec(name, l=None, i=0, n=1):
        o = VOFF[name] if l is None else VOFF[(name, l)]
        return vecs[:, o + i:o + i + n]

    def cast_rows(dst, src, r0, r1, buf):
        r = r0
        while r < r1:
            n = min(512, r1 - r)
            dma("pool", dst[r:r + n, :], src[r:r + n, :], writes=[buf], nobar=True)
            r += n

    def cast_weights(l_, names):
        for n in names:
            o, r = WOFF[n]
            cast_rows(wb[l_], wl_in[l_], o, o + r, wbuf[(l_, n)])

    cast_weights(0, ("win", "wkv", "sgw"))

    modb = Buf("mod")

    class ModJob:
        def __init__(self, l):
            self.l = l

        def start(self):
            l = self.l
            self.scT = S.sb([128, 8, 3], F32, "scT")
            self.scb = Buf("scT")
            cTv = vecs[:, VOFF["cT"]:VOFF["cT"] + 24]
            op("act", lambda e: e.activation(out=self.scT[:].rearrange("p k r -> p (k r)"), in_=cTv, func=AF.Silu), reads=[vecsb], writes=[self.scb])
            self.was = [S.sb([128, 8, 256], F32, "wada") for _ in range(2)]
            self.wabs = [Buf("wa%d" % i) for i in range(2)]
            self.mtm = S.sb([3, 256], F32, "mtm")
            self.mtmb = Buf("mtm")
            self.src = wada_in[l * ADA_ROWS:(l + 1) * ADA_ROWS, :].rearrange("r c -> (r c)").rearrange("(p k n) -> p k n", p=128, k=8)

        def group(self, g):
            l = self.l
            wa, wab = self.was[g % 2], self.wabs[g % 2]
            dma("sp", wa[:], self.src[:, :, g * 256:(g + 1) * 256], writes=[wab])
            bk, bb = ps()
            for kc in range(8):
                op("pe", lambda e: e.matmul(bk[0:3, 0:256], lhsT=self.scT[:, kc, :], rhs=wa[:, kc, :], start=(kc == 0), stop=(kc == 7)), reads=[wab, self.scb], writes=[bb])
            op("act", lambda e: e.activation(out=self.mtm[0:3, :], in_=bk[0:3, 0:256], func=AF.Copy), reads=[bb], writes=[self.mtmb])
            bt, btb = ps()
            for m in range(2):
                op("pe", lambda e: e.transpose(bt[:, m * 3:(m + 1) * 3], self.mtm[0:3, m * 128:(m + 1) * 128], ident_f[0:3, 0:3]), reads=[self.mtmb, constb], writes=[btb])
            o = VOFF[("b_ada", l)] + 2 * g
            op("dve", lambda e: e.tensor_tensor(out=modv[:, l, 2 * g:2 * g + 2, :], in0=bt[:, 0:6].rearrange("p (m r) -> p m r", r=3),
                                                in1=vecs[:, o:o + 2].unsqueeze(2).to_broadcast([128, 2, 3]), op=ALU.add), reads=[btb, vecsb], writes=[modb])

        def finish(self):
            l = self.l
            t1 = S.sb([128, 4], F32, "t1")
            t2 = S.sb([128, 4], F32, "t2")
            t3 = S.sb([128, 4], F32, "t3")
            tb = Buf("clt")
            for which, (nm, sc0) in enumerate((("norm1", 8), ("norm2", 32))):
                for r in range(3):
                    op("dve", lambda e: e.scalar_tensor_tensor(out=gmv[:, l, which, :, r], in0=modv[:, l, sc0:sc0 + 8, r], scalar=1.0,
                                                               in1=vec(nm, l, 0, 8), op0=ALU.add, op1=ALU.mult), reads=[modb, vecsb], writes=[modb])
            op("act", lambda e: e.activation(out=t1[:], in_=vec("lru_lam", l, 0, 4), func=AF.Exp, scale=-1.0), reads=[vecsb], writes=[tb])
            op("dve", lambda e: e.tensor_scalar(out=t2[:], in0=t1[:], scalar1=2.0, scalar2=None, op0=ALU.add), reads=[tb], writes=[tb])
            op("dve", lambda e: e.reciprocal(out=t2[:], in_=t2[:]), reads=[tb], writes=[tb])
            op("dve", lambda e: e.tensor_tensor(out=t1[:], in0=t1[:], in1=t2[:], op=ALU.mult), reads=[tb], writes=[tb])
            op("dve", lambda e: e.tensor_tensor(out=t2[:], in0=t1[:], in1=t1[:], op=ALU.mult), reads=[tb], writes=[tb])
            op("dve", lambda e: e.tensor_scalar(out=t3[:], in0=t2[:], scalar1=0.2, scalar2=1.0 / 3.0, op0=ALU.mult, op1=ALU.add), reads=[tb], writes=[tb])
            op("dve", lambda e: e.tensor_tensor(out=t3[:], in0=t3[:], in1=t2[:], op=ALU.mult), reads=[tb], writes=[tb])
            op("dve", lambda e: e.tensor_scalar(out=t3[:], in0=t3[:], scalar1=1.0, scalar2=None, op0=ALU.add), reads=[tb], writes=[tb])
            op("dve", lambda e: e.tensor_tensor(out=t3[:], in0=t3[:], in1=t1[:], op=ALU.mult), reads=[tb], writes=[tb])
            op("dve", lambda e: e.tensor_scalar(out=clv[:, l, 0:4], in0=t3[:], scalar1=-16.0, scalar2=None, op0=ALU.mult), reads=[tb], writes=[modb])
            op("dve", lambda e: e.tensor_scalar(out=clv[:, l, 4:8], in0=t3[:], scalar1=-32.0, scalar2=None, op0=ALU.mult), reads=[tb], writes=[modb])

    modb = Buf("mod")
    S.push()
    mj0 = ModJob(0)
    mj0.start()
    for g_ in range(24):
        mj0.group(g_)
    mj0.finish()
    S.pop()

    def tap(name, src_ap, bufs):
        if name in tap_out:
            dma("pool", tap_out[name], src_ap, reads=bufs)

    if "modv" in tap_out:
        tap("modv", modv[:].rearrange("p l m r -> p (l m r)"), [modb])
        tap("gmv", gmv[:].rearrange("p l w m r -> p (l w m r)"), [modb])
        tap("clv", clv[:].rearrange("p l m -> p (l m)"), [modb])

    def rstd_from_ss(bk, bb, N, scale, rs, rsb):
        op("act", lambda e: e.activation(out=rs[:, :N], in_=bk[:, :N], func=AF.Ln, bias=epsc[:, 0:1], scale=scale), reads=[bb, constb], writes=[rsb])
        op("act", lambda e: e.activation(out=rs[:, :N], in_=rs[:, :N], func=AF.Exp, scale=-0.5), reads=[rsb], writes=[rsb])

    epsc = S.sb([128, 1], F32, "eps")
    op("pool", lambda e: e.memset(epsc[:], EPS), writes=[constb])
    onec = S.sb([128, 1], F32, "onec")
    op("pool", lambda e: e.memset(onec[:], 1.0), writes=[constb])

    def gelu(eng_mul, out_ap, in_ap, tmp, tmpb, N, rd, wr):
        p0 = in_ap.base_partition if hasattr(in_ap, "base_partition") else 0
        a = tmp[:, 0, :N]
        b_ = tmp[:, 1, :N]
        op("act", lambda e: e.activation(out=a, in_=in_ap, func=AF.Square), reads=rd, writes=[tmpb])
        op("dve", lambda e: e.tensor_scalar(out=a, in0=a, scalar1=0.044715, scalar2=1.0, op0=ALU.mult, op1=ALU.add), reads=[tmpb], writes=[tmpb])
        op("dve", lambda e: e.tensor_tensor(out=a, in0=a, in1=in_ap, op=ALU.mult), reads=[tmpb] + rd, writes=[tmpb])
        op("act", lambda e: e.activation(out=b_, in_=a, func=AF.Sigmoid, scale=GELU_C), reads=[tmpb], writes=[tmpb])
        op("dve", lambda e: e.tensor_tensor(out=out_ap, in0=b_, in1=in_ap, op=ALU.mult), reads=[tmpb] + rd, writes=wr)

    def run_rr(gens, tag=""):
        gens = list(gens)
        if DBG.get("norr") or DBG.get("norr_" + tag):
            for g_ in gens:
                for _ in g_:
                    pass
            return
        while gens:
            nxt = []
            for g_ in gens:
                try:
                    next(g_)
                    nxt.append(g_)
                except StopIteration:
                    pass
            gens = nxt

    def gelu_g(out_ap, in_ap, a, b_, tmpb, rd, wr):
        op("act", lambda e: e.activation(out=a, in_=in_ap, func=AF.Square), reads=rd, writes=[tmpb])
        yield
        op("dve", lambda e: e.tensor_scalar(out=a, in0=a, scalar1=0.044715, scalar2=1.0, op0=ALU.mult, op1=ALU.add), reads=[tmpb], writes=[tmpb])
        yield
        op("dve", lambda e: e.tensor_tensor(out=a, in0=a, in1=in_ap, op=ALU.mult), reads=[tmpb] + rd, writes=[tmpb])
        yield
        op("act", lambda e: e.activation(out=b_, in_=a, func=AF.Sigmoid, scale=GELU_C), reads=[tmpb], writes=[tmpb])
        yield
        op("dve", lambda e: e.tensor_tensor(out=out_ap, in0=b_, in1=in_ap, op=ALU.mult), reads=[tmpb] + rd, writes=wr)
        yield

    def load_x(b, ti, l, xt, xtb):
        t0, N = TILES[ti]
        src = xT_in if l == 0 else xs
        dma("sp", xt[:, :, :N], src[b, :, :, t0:t0 + N], reads=[xbuf[(b, ti)]], writes=[xtb])

    def adaln_g(l, which, r, xt, xtb, N, hT, hTb, sq, sqb, rs, rsb, tmp, tmpb):
        bk, bb = ps()
        for c in range(8):
            op("act", lambda e: e.activation(out=sq[:, c % 2, :N], in_=xt[:, c, :N], func=AF.Square), reads=[xtb], writes=[sqb[c % 2]])
            op("pe", lambda e: e.matmul(bk[:, :N], lhsT=ones[:], rhs=sq[:, c % 2, :N], start=(c == 0), stop=(c == 7)), reads=[sqb[c % 2], constb], writes=[bb])
            yield
        op("act", lambda e: e.activation(out=rs[:, :N], in_=bk[:, :N], func=AF.Ln, bias=epsc[:, 0:1], scale=1.0 / D), reads=[bb, constb], writes=[rsb])
        yield
        op("act", lambda e: e.activation(out=rs[:, :N], in_=rs[:, :N], func=AF.Exp, scale=-0.5), reads=[rsb], writes=[rsb])
        yield
        sh0 = 0 if which == 0 else 24
        for c in range(8):
            op("dve", lambda e: e.scalar_tensor_tensor(out=tmp[:, c % 2, :N], in0=xt[:, c, :N], scalar=gmv[:, l, which, c, r:r + 1],
                                                       in1=rs[:, :N], op0=ALU.mult, op1=ALU.mult), reads=[xtb, rsb, modb], writes=[tmpb[c % 2]])
            op("act", lambda e: e.activation(out=hT[:, c, :N], in_=tmp[:, c % 2, :N], func=AF.Identity, bias=modv[:, l, sh0 + c, r:r + 1]),
               reads=[tmpb[c % 2], modb], writes=[hTb])
            yield

    def adaln(l, which, r, xt, xtb, N, hT, hTb, sq, sqb, rs, rsb, tmp, tmpb):
        bk, bb = ps()
        for c in range(8):
            op("act", lambda e: e.activation(out=sq[:, c % 2, :N], in_=xt[:, c, :N], func=AF.Square), reads=[xtb], writes=[sqb[c % 2]])
            op("pe", lambda e: e.matmul(bk[:, :N], lhsT=ones[:], rhs=sq[:, c % 2, :N], start=(c == 0), stop=(c == 7)), reads=[sqb[c % 2], constb], writes=[bb])
        rstd_from_ss(bk, bb, N, 1.0 / D, rs, rsb)
        sh0 = 0 if which == 0 else 24
        for c in range(8):
            op("dve", lambda e: e.scalar_tensor_tensor(out=tmp[:, c % 2, :N], in0=xt[:, c, :N], scalar=gmv[:, l, which, c, r:r + 1],
                                                       in1=rs[:, :N], op0=ALU.mult, op1=ALU.mult), reads=[xtb, rsb, modb], writes=[tmpb[c % 2]])
            op("act", lambda e: e.activation(out=hT[:, c, :N], in_=tmp[:, c % 2, :N], func=AF.Identity, bias=modv[:, l, sh0 + c, r:r + 1]),
               reads=[tmpb[c % 2], modb], writes=[hTb])

    niter = [0]
    for b in range(2):
        for l in range(DEPTH):
            last = l == DEPTH - 1
            S.push()
            yrs = S.sb([128, 4, TOK], BF16, "yrs")
            yrsb = Buf("yrs")
            xr = S.sb([128, 2, TOK], F32, "xr")
            xrb = Buf("xr")
            ggr = S.sb([128, 2, TOK], BF16, "ggr")
            ggrb = Buf("ggr")
            S.push()
            KT = S.sb([96, 8, TOK], BF16, "KT")
            KTb = Buf("KT")
            KTrb = Buf("KTr")
            VA = S.sb([128, 18, 4, 192], BF16, "VA")
            VAb = Buf("VA")
            qnT = S.sb([128, 2, TOK], BF16, "qnT")
            qnb = Buf("qnT")
            op("dve", lambda e: e.memset(VA[:, :, :, 64:128], 1.0), writes=[VAb])

            S.push()
            xt = S.sb([128, 8, 512], F32, "xt")
            xtb = Buf("xt")
            sq = S.sb([128, 2, 512], BF16, "sq")
            sqb = [Buf("sq0"), Buf("sq1")]
            rs = S.sb([128, 512], F32, "rs")
            rsb = Buf("rs")
            tmp = S.sb([128, 2, 512], F32, "tmp")
            tmpb = [Buf("tmp0"), Buf("tmp1")]
            hT = S.sb([128, 8, 512], BF16, "hT")
            hTb = Buf("hT")
            slab = [S.sb([128, 8, 512], BF16, "slab") for _ in range(2)]
            slabb = [Buf("slab0"), Buf("slab1")]
            wkv = S.sb([128, 1024], BF16, "wkv")
            wkvb = Buf("wkv")
            sgw = S.sb([128, 4, 128], BF16, "sgw")
            sgwb = Buf("sgw")
            kvn = S.sb([128, 512], BF16, "kvn")
            kvnb = Buf("kvn")
            rq = S.sb([128, 512], F32, "rq")
            rqb = Buf("rq")
            gt = S.sb([128, 2, 512], F32, "gt")
            gtb = Buf("gt")
            rt = gt
            rtb = gtb
            ga = S.sb([128, 4, 256], F32, "ga")
            gb = S.sb([128, 4, 256], F32, "gb")
            guv = S.sb([128, 4, 512], F32, "guv")
            vgb = S.sb([128, 4, 256], BF16, "vgb")
            spn = S.sb([128, 4, 256], BF16, "spn")
            st = S.sb([128, 32], F32, "st")
            sgb = Buf("sgu")
            ropet = S.sb([128, 2, 512], F32, "ropet")
            ropeb = Buf("ropet")
            dma("sp", wkv[:], wview(l, "wkv", "(p n) -> p n", p=128), reads=[wbuf[(l, "wkv")]], writes=[wkvb])
            dma("sp", sgw[:], wview(l, "sgw", "(p g n) -> p g n", p=128, g=4), reads=[wbuf[(l, "sgw")]], writes=[sgwb])
            winv = wview(l, "win", "(p k n) -> p k n", p=128, k=8)
            nslab = [0]

            def get_slab(g):
                i = nslab[0] % 2
                nslab[0] += 1
                dma("sp", slab[i][:], winv[:, :, g * 512:(g + 1) * 512], reads=[wbuf[(l, "win")]], writes=[slabb[i]])
                return slab[i], slabb[i]

            def group1_g(ti):
                t0, N = TILES[ti]
                isctx = ti == 0
                sl, slb = get_slab(0)
                bq = [ps(), ps()]
                for mc in range(2):
                    for kc in range(8):
                        op("pe", lambda e: e.matmul(bq[mc][0][:, :N], lhsT=sl[:, kc, mc * 128:(mc + 1) * 128], rhs=hT[:, kc, :N], start=(kc == 0), stop=(kc == 7)),
                           reads=[slb, hTb], writes=[bq[mc][1]])
                        yield
                bss, bssb = ps()
                for mc in range(2):
                    op("act", lambda e: e.activation(out=sq[:, mc, :N], in_=bq[mc][0][:, :N], func=AF.Square), reads=[bq[mc][1]], writes=[sqb[mc]])
                    yield
                    op("pe", lambda e: e.matmul(bss[:, :N], lhsT=ones[:], rhs=sq[:, mc, :N], start=(mc == 0), stop=(mc == 1)), reads=[sqb[mc], constb], writes=[bssb])
                    yield
                rstd_from_ss(bss, bssb, N, 1.0 / 256, rq, rqb)
                yield
                for mc in range(2):
                    op("dve", lambda e: e.scalar_tensor_tensor(out=qnT[:, mc, t0:t0 + N], in0=bq[mc][0][:, :N], scalar=vec("qn_g", l, mc), in1=rq[:, :N],
                                                               op0=ALU.mult, op1=ALU.mult), reads=[bq[mc][1], rqb, vecsb], writes=[qnb])
                    yield
                bkv, bkvb = ps()
                for kc in range(8):
                    op("pe", lambda e: e.matmul(bkv[:, :N], lhsT=sl[:, kc, 256:384], rhs=hT[:, kc, :N], start=(kc == 0), stop=(kc == 7)), reads=[slb, hTb], writes=[bkvb])
                    yield
                bkr, bkrb = ps()
                for kc in range(8):
                    op("pe", lambda e: e.matmul(bkr[64:96, :N], lhsT=sl[:, kc, 384:416], rhs=hT[:, kc, :N], start=(kc == 0), stop=(kc == 7)), reads=[slb, hTb], writes=[bkrb])
                    yield
                if not isctx:
                    bkp, bkpb = ps()
                    for kc in range(8):
                        op("pe", lambda e: e.matmul(bkp[64:96, :N], lhsT=sl[:, kc, 416:448], rhs=hT[:, kc, :N], start=(kc == 0), stop=(kc == 7)), reads=[slb, hTb], writes=[bkpb])
                        yield
                op("act", lambda e: e.activation(out=sq[:, 0, :N], in_=bkv[:, :N], func=AF.Square), reads=[bkvb], writes=[sqb[0]])
                yield
                bss, bssb = ps()
                op("pe", lambda e: e.matmul(bss[:, :N], lhsT=ones[:], rhs=sq[:, 0, :N], start=True, stop=True), reads=[sqb[0], constb], writes=[bssb])
                yield
                rstd_from_ss(bss, bssb, N, 1.0 / 128, rq, rqb)
                yield
                op("dve", lambda e: e.scalar_tensor_tensor(out=kvn[:, :N], in0=bkv[:, :N], scalar=vec("kvn_g", l, 0), in1=rq[:, :N], op0=ALU.mult, op1=ALU.mult),
                   reads=[bkvb, rqb, vecsb], writes=[kvnb])
                yield
                if isctx:
                    op("dve", lambda e: e.tensor_copy(out=KT[64:96, :, t0:t0 + N], in_=bkr[64:96, :N].unsqueeze(1).to_broadcast([32, 8, N])), reads=[bkrb], writes=[KTrb])
                    yield
                else:
                    p0 = t0 - CTX
                    op("dve", lambda e: e.tensor_tensor(out=rt[64:96, 0, :N], in0=bkr[64:96, :N], in1=ropet[64:96, 0, :N], op=ALU.mult), reads=[bkrb, ropeb], writes=[rtb])
                    yield
                    op("dve", lambda e: e.tensor_tensor(out=rt[64:96, 1, :N], in0=bkp[64:96, :N], in1=ropet[64:96, 1, :N], op=ALU.mult), reads=[bkpb, ropeb], writes=[rtb])
                    yield
                    op("dve", lambda e: e.tensor_tensor(out=KT[64:96, :, t0:t0 + N], in0=rt[64:96, 0, :N].unsqueeze(1).to_broadcast([32, 8, N]),
                                                        in1=rt[64:96, 1, :N].unsqueeze(1).to_broadcast([32, 8, N]), op=ALU.add), reads=[rtb], writes=[KTrb])
                    yield
                for hp in range(4):
                    bk, bb = ps()
                    op("pe", lambda e: e.matmul(bk[:, :N], lhsT=wkv[:, hp * 128:(hp + 1) * 128], rhs=kvn[:, :N], start=True, stop=True), reads=[wkvb, kvnb], writes=[bb])
                    yield
                    op("act", lambda e: e.activation(out=KT[0:64, 2 * hp, t0:t0 + N], in_=bk[0:64, :N], func=AF.Copy), reads=[bb], writes=[KTb])
                    yield
                    op("act", lambda e: e.activation(out=KT[0:64, 2 * hp + 1, t0:t0 + N], in_=bk[64:128, :N], func=AF.Copy), reads=[bb], writes=[KTb])
                    yield
                for sbk in range(N // 128):
                    kt = (t0 + sbk * 128) // 128
                    bk, bb = ps()
                    op("pe", lambda e: e.matmul(bk[:, :], lhsT=kvn[:, sbk * 128:(sbk + 1) * 128], rhs=wkv[:, 512:1024], start=True, stop=True), reads=[wkvb, kvnb], writes=[bb])
                    yield
                    bk4 = bk[:, :].rearrange("p (j two d) -> p j two d", two=2, d=64)
                    op("dve", lambda e: e.tensor_copy(out=VA[:, kt, :, 0:64], in_=bk4[:, :, 0, :]), reads=[bb], writes=[VAb])
                    yield
                    op("dve", lambda e: e.tensor_copy(out=VA[:, kt, :, 128:192], in_=bk4[:, :, 1, :]), reads=[bb], writes=[VAb])
                    yield

            def G(ti):
                t0_, N_ = TILES[ti]
                r_ = 2 if ti == 0 else b
                yield from adaln_g(l, 0, r_, xt, xtb, N_, hT, hTb, sq, sqb, rs, rsb, tmp, tmpb)
                yield from group1_g(ti)

            pool_ids[:] = [0, 1, 2, 3, 4, 5, 6]
            for ti, (t0, N) in enumerate(TILES):
                isctx = ti == 0
                r = 2 if isctx else b
                if ti == 0:
                    load_x(b, ti, l, xt, xtb)
                    for _ in G(0):
                        pass
                if b == 0 and l == 0 and ti == 1:
                    cast_weights(0, ("lru", "wq", "wout", "wgu", "wo2"))
                if b == 0 and l == 0 and ti == 3:
                    cast_weights(1, ("win", "wkv", "sgw", "lru", "wq", "wout", "wgu", "wo2"))
                if ti + 1 < len(TILES):
                    t0n, Nn = TILES[ti + 1]
                    load_x(b, ti + 1, l, xt, xtb)
                    dma("sp", ropet[64:96, :, :Nn], rope_in[64:96, :, t0n - CTX:t0n - CTX + Nn], writes=[ropeb])
                sl, slb = get_slab(1)
                for mc in range(4):
                    bk, bb = ps()
                    for kc in range(8):
                        op("pe", lambda e: e.matmul(bk[:, :N], lhsT=sl[:, kc, mc * 128:(mc + 1) * 128], rhs=hT[:, kc, :N], start=(kc == 0), stop=(kc == 7)), reads=[slb, hTb], writes=[bb])
                    if mc < 2:
                        op("act", lambda e: e.activation(out=xr[:, mc, t0:t0 + N], in_=bk[:, :N], func=AF.Copy), reads=[bb], writes=[xrb])
                    else:
                        op("act", lambda e: e.activation(out=ggr[:, mc - 2, t0:t0 + N], in_=bk[:, :N], func=AF.Gelu_apprx_tanh), reads=[bb], writes=[ggrb])
                sl, slb = get_slab(2)
                nsb = N // 128
                AB = []
                for sbk in range(nsb):
                    bk, bb = ps()
                    for kc in range(8):
                        op("pe", lambda e: e.matmul(bk[:, :], lhsT=hT[:, kc, sbk * 128:(sbk + 1) * 128], rhs=sl[:, kc, :], start=(kc == 0), stop=(kc == 7)), reads=[slb, hTb], writes=[bb])
                    AB.append((bk, bb))
                for sbk in range(nsb):
                    op("act", lambda e: e.activation(out=guv[:, sbk, :], in_=AB[sbk][0][:, :], func=AF.Gelu_apprx_tanh), reads=[AB[sbk][1]], writes=[sgb])
                def sgu_tail():
                    gvv = guv[:, :nsb, 256:512]
                    guu = guv[:, :nsb, 0:256]
                    wk = ga[:, :nsb, :]
                    spt = gb[:, :nsb, :]
                    g4 = lambda ap: ap.rearrange("p s (g c) -> p s g c", g=4)
                    op("dve", lambda e: e.tensor_tensor(out=wk, in0=gvv, in1=gvv, op=ALU.mult), reads=[sgb], writes=[sgb])
                    yield
                    r16 = st[:, 0:4 * nsb].rearrange("p (s g) -> p s g", g=4)
                    op("dve", lambda e: e.tensor_reduce(out=r16, in_=g4(wk), axis=AX.X, op=ALU.add), reads=[sgb], writes=[sgb])
                    yield
                    op("act", lambda e: e.activation(out=st[:, 0:4 * nsb], in_=st[:, 0:4 * nsb], func=AF.Ln, bias=epsc[:, 0:1], scale=1.0 / 64), reads=[sgb, constb], writes=[sgb])
                    yield
                    op("act", lambda e: e.activation(out=st[:, 0:4 * nsb], in_=st[:, 0:4 * nsb], func=AF.Exp, scale=-0.5), reads=[sgb], writes=[sgb])
                    yield
                    op("dve", lambda e: e.tensor_tensor(out=g4(wk), in0=g4(gvv), in1=r16.unsqueeze(3).to_broadcast([128, nsb, 4, 64]), op=ALU.mult), reads=[sgb], writes=[sgb])
                    yield
                    op("dve", lambda e: e.tensor_tensor(out=vgb[:, :nsb, :], in0=wk, in1=sgn[:, l * 256:(l + 1) * 256].unsqueeze(1).to_broadcast([128, nsb, 256]), op=ALU.mult),
                       reads=[sgb, constb], writes=[sgb])
                    yield
                    npair = (nsb + 1) // 2
                    bias_bc = vecs[:, VOFF[("sgu_b", l)]:VOFF[("sgu_b", l)] + 4].unsqueeze(1).unsqueeze(3).to_broadcast([128, 2, 4, 64])
                    for pr in range(npair):
                        bs, bsb = ps()
                        for s2 in range(2):
                            sbk = pr * 2 + s2
                            for g in range(4):
                                op("pe", lambda e: e.matmul(bs[:, s2 * 256 + g * 64:s2 * 256 + (g + 1) * 64], lhsT=sgw[:, g, :], rhs=vgb[:, sbk, g * 64:(g + 1) * 64], start=True, stop=True),
                                   reads=[sgwb, sgb], writes=[bsb])
                                yield
                        sp2 = gb[:, pr * 2:pr * 2 + 2, :]
                        op("dve", lambda e: e.tensor_tensor(out=g4(sp2), in0=bs[:, :].rearrange("p (s g c) -> p s g c", s=2, g=4), in1=bias_bc, op=ALU.add), reads=[bsb, vecsb, sgb], writes=[sgb])
                        yield
                        op("dve", lambda e: e.tensor_tensor(out=sp2, in0=sp2, in1=guv[:, pr * 2:pr * 2 + 2, 0:256], op=ALU.mult), reads=[sgb], writes=[sgb])
                        yield
                    op("dve", lambda e: e.tensor_tensor(out=wk, in0=spt, in1=spt, op=ALU.mult), reads=[sgb], writes=[sgb])
                    yield
                    op("dve", lambda e: e.tensor_reduce(out=st[:, 16:16 + nsb], in_=wk, axis=AX.X, op=ALU.add), reads=[sgb], writes=[sgb])
                    yield
                    op("act", lambda e: e.activation(out=st[:, 16:16 + nsb], in_=st[:, 16:16 + nsb], func=AF.Ln, bias=epsc[:, 0:1], scale=1.0 / 256), reads=[sgb, constb], writes=[sgb])
                    yield
                    op("act", lambda e: e.activation(out=st[:, 16:16 + nsb], in_=st[:, 16:16 + nsb], func=AF.Exp, scale=-0.5), reads=[sgb], writes=[sgb])
                    yield
                    op("dve", lambda e: e.tensor_tensor(out=spn[:, :nsb, :], in0=spt, in1=st[:, 16:16 + nsb].unsqueeze(2).to_broadcast([128, nsb, 256]), op=ALU.mult), reads=[sgb], writes=[sgb])
                    yield
                    for sbk in range(nsb):
                        tk = t0 + sbk * 128
                        for mc in range(2):
                            op("pe", lambda e: e.transpose(tbank[:, mc * 128:(mc + 1) * 128], spn[:, sbk, mc * 128:(mc + 1) * 128], ident[:]), reads=[sgb, constb], writes=[tbankb])
                            yield
                            op("dve", lambda e: e.tensor_scalar(out=yrs[:, 2 + mc, tk:tk + 128], in0=tbank[:, mc * 128:(mc + 1) * 128], scalar1=vec("out_norm", l, 6 + mc),
                                                                scalar2=None, op0=ALU.mult), reads=[tbankb, vecsb], writes=[yrsb])
                            yield

                if ti + 1 < len(TILES):
                    t0n, Nn = TILES[ti + 1]
                    run_rr([sgu_tail(), G(ti + 1)], "p1")
                else:
                    for _ in sgu_tail():
                        pass
            S.pop()

            if stage >= 2:
                S.push()
                lw = S.sb([128, 2, 2, 2, 128], BF16, "lw")
                lwb = Buf("lw")
                dma("sp", lw[:], wview(l, "lru", "(p a d c n) -> p a d c n", p=128, a=2, d=2, c=2), reads=[wbuf[(l, "lru")]], writes=[lwb])
                xc2 = [S.sb([128, TOK], F32, "xc") for _ in range(2)]
                xcb2 = [Buf("xc0"), Buf("xc1")]
                xcbf2 = [S.sb([128, TOK], BF16, "xcbf") for _ in range(2)]
                xcbfb2 = [Buf("xcbf0"), Buf("xcbf1")]
                aa = S.sb([128, TOK], F32, "aa")
                aab = Buf("aa")
                bbt = S.sb([128, TOK], F32, "bbt")
                bbtb = Buf("bbt")
                hs = S.sb([128, TOK], F32, "hs")
                hsb = Buf("hs")
                rec = xr
                recb = xrb
                ltm = S.sb([128, TOK], F32, "ltm")
                ltmb = Buf("ltm")
                mj1 = None
                if b == 0 and l == 0 and DEPTH > 1:
                    mj1 = ModJob(1)
                    mj1.start()
                for c in range(2):
                    xc, xcb, xcbf, xcbfb = xc2[c], xcb2[c], xcbf2[c], xcbfb2[c]
                    cw = lambda j: vec("conv_w", l, c * 4 + j)
                    for (s0, s1) in ((0, CTX), (CTX, TOK)):
                        op("dve", lambda e: e.tensor_scalar(out=xc[:, s0:s1], in0=xr[:, c, s0:s1], scalar1=cw(2), scalar2=vec("conv_b", l, c), op0=ALU.mult, op1=ALU.add),
                           reads=[xrb, vecsb], writes=[xcb])
                        for j, sh in ((0, -2), (1, -1), (3, 1)):
                            d0, d1 = max(s0, s0 - sh), min(s1, s1 - sh)
                            op("dve", lambda e: e.scalar_tensor_tensor(out=xc[:, d0:d1], in0=xr[:, c, d0 + sh:d1 + sh], scalar=cw(j), in1=xc[:, d0:d1], op0=ALU.mult, op1=ALU.add),
                               reads=[xrb, xcb, vecsb], writes=[xcb])
                    op("act", lambda e: e.activation(out=xcbf[:, :], in_=xc[:, :], func=AF.Copy), reads=[xcb], writes=[xcbfb])
                for c in range(2):
                    xc, xcb, xcbf, xcbfb = xc2[c], xcb2[c], xcbf2[c], xcbfb2[c]
                    for d in range(2):
                        for (t0, N) in TILES:
                            br, brb = ps()
                            op("pe", lambda e: e.matmul(br[:, :N], lhsT=lw[:, 0, d, c, :], rhs=xcbf[:, t0:t0 + N], start=True, stop=True), reads=[lwb, xcbfb], writes=[brb])
                            bi, bib = ps()
                            op("pe", lambda e: e.matmul(bi[:, :N], lhsT=lw[:, 1, d, c, :], rhs=xcbf[:, t0:t0 + N], start=True, stop=True), reads=[lwb, xcbfb], writes=[bib])
                            op("act", lambda e: e.activation(out=aa[:, t0:t0 + N], in_=br[:, :N], func=AF.Sigmoid, bias=vec("lru_b_r", l, d * 2 + c)), reads=[brb, vecsb], writes=[aab])
                            op("act", lambda e: e.activation(out=bbt[:, t0:t0 + N], in_=bi[:, :N], func=AF.Sigmoid, bias=vec("lru_b_i", l, d * 2 + c)), reads=[bib, vecsb], writes=[bbtb])
                        op("act", lambda e: e.activation(out=ltm[:, :], in_=aa[:, :], func=AF.Exp, scale=clv[:, l, 4 + d * 2 + c:4 + d * 2 + c + 1]), reads=[aab, modb], writes=[ltmb])
                        op("act", lambda e: e.activation(out=aa[:, :], in_=aa[:, :], func=AF.Exp, scale=clv[:, l, d * 2 + c:d * 2 + c + 1]), reads=[aab, modb], writes=[aab])
                        op("dve", lambda e: e.tensor_tensor(out=bbt[:, :], in0=bbt[:, :], in1=xc[:, :], op=ALU.mult), reads=[bbtb, xcb], writes=[bbtb])
                        op("act", lambda e: e.activation(out=ltm[:, :], in_=ltm[:, :], func=AF.Sqrt, bias=onec[:, 0:1], scale=-1.0), reads=[ltmb, constb], writes=[ltmb])
                        op("dve", lambda e: e.tensor_tensor(out=bbt[:, :], in0=bbt[:, :], in1=ltm[:, :], op=ALU.mult), reads=[bbtb, ltmb], writes=[bbtb])
                        if mj1 is not None:
                            for g_ in range(6):
                                mj1.group((c * 2 + d) * 6 + g_)
                        if d == 0:
                            op("dve", lambda e: e.tensor_tensor_scan(out=hs[:, :], data0=aa[:, :], data1=bbt[:, :], initial=0.0, op0=ALU.mult, op1=ALU.add),
                               reads=[aab, bbtb], writes=[hsb])
                        else:
                            op("dve", lambda e: e.tensor_tensor_scan(out=bbt[:, 0:CTX][:, ::-1], data0=aa[:, 0:CTX][:, ::-1], data1=bbt[:, 0:CTX][:, ::-1], initial=0.0,
                                                                     op0=ALU.mult, op1=ALU.add), reads=[aab, bbtb], writes=[bbtb])
                            op("dve", lambda e: e.tensor_tensor_scan(out=bbt[:, CTX:TOK][:, ::-1], data0=aa[:, CTX:TOK][:, ::-1], data1=bbt[:, CTX:TOK][:, ::-1], initial=bbt[:, 0:1],
                                                                     op0=ALU.mult, op1=ALU.add), reads=[aab, bbtb], writes=[bbtb])
                            op("dve", lambda e: e.tensor_tensor(out=hs[:, :], in0=hs[:, :], in1=bbt[:, :], op=ALU.add), reads=[hsb, bbtb], writes=[hsb])
                    op("dve", lambda e: e.tensor_tensor(out=rec[:, c, :], in0=hs[:, :], in1=ggr[:, c, :], op=ALU.mult), reads=[hsb, ggrb], writes=[recb])
                if "rec" in tap_out and b == 0 and l == 0:
                    tap("rec", rec[:].rearrange("p c t -> p (c t)"), [recb])
                if mj1 is not None:
                    mj1.finish()
                lsq = bbt[:, 0:512].bitcast(BF16).rearrange("p (c t) -> p c t", c=2)
                lsqb = [bbtb, bbtb]
                lrs = ltm
                lrsb = ltmb
                for (t0, N) in TILES:
                    bss, bssb = ps()
                    for c in range(2):
                        op("act", lambda e: e.activation(out=lsq[:, c, :N], in_=rec[:, c, t0:t0 + N], func=AF.Square), reads=[recb], writes=[lsqb[c]])
                        op("pe", lambda e: e.matmul(bss[:, :N], lhsT=ones[:], rhs=lsq[:, c, :N], start=(c == 0), stop=(c == 1)), reads=[lsqb[c], constb], writes=[bssb])
                    rstd_from_ss(bss, bssb, N, 1.0 / 256, lrs, lrsb)
                    for c in range(2):
                        op("dve", lambda e: e.scalar_tensor_tensor(out=yrs[:, c, t0:t0 + N], in0=rec[:, c, t0:t0 + N], scalar=vec("out_norm", l, 4 + c), in1=lrs[:, :N],
                                                                   op0=ALU.mult, op1=ALU.mult), reads=[recb, lrsb, vecsb], writes=[yrsb])
                S.pop()

            if b == 0 and l == 0:
                S.push()
                stg = S.sb([128, TOK], F32, "stg")
                stgb = Buf("stg")
                for nm, src, sb_, P_, C_, T_ in (("qnT", qnT, qnb, 128, 2, TOK), ("KT", KT, KTb, 96, 8, TOK), ("yrs", yrs, yrsb, 128, 4, TOK)):
                    if nm in tap_out:
                        for c_ in range(C_):
                            op("dve", lambda e: e.tensor_copy(out=stg[0:P_, 0:T_], in_=src[0:P_, c_, :]), reads=[sb_], writes=[stgb])
                            dma("pool", tap_out[nm][:, c_ * T_:(c_ + 1) * T_], stg[0:P_, 0:T_], reads=[stgb])
                S.pop()
                if "xr" in tap_out:
                    tap("xr", xr[:].rearrange("p c t -> p (c t)"), [xrb])
                    tap("ggr", ggr[:].rearrange("p c t -> p (c t)"), [ggrb])

            ya = None
            if stage >= 3:
                S.push()
                ya = xr[:].rearrange("p c t -> p (c t)").bitcast(BF16).rearrange("p (c t) -> p c t", c=4)
                yab = xrb
                wq = S.sb([128, 2, 8, 128], BF16, "wq")
                wqb_ = Buf("wq")
                dma("sp", wq[:], wview(l, "wq", "(p k h n) -> p k h n", p=128, k=2, h=8), reads=[wbuf[(l, "wq")]], writes=[wqb_])
                QT2 = [S.sb([96, 8, 512], BF16, "QT") for _ in range(2)]
                QTb2 = [Buf("QT0"), Buf("QT1")]
                qrt = S.sb([96, 2, 512], F32, "qrt")
                qrtb = Buf("qrt")
                PT = [S.sb([128, 2, 512], BF16, "PT") for _ in range(3)]
                PTb = [Buf("PT%d" % i) for i in range(3)]
                pool_ids[:] = [4, 7]
                ropeq = S.sb([128, 2, 512], F32, "ropeq")
                ropeqb = Buf("ropeq")
                Rr = S.sb([128, 512], F32, "Rr")
                Rrb = Buf("Rr")
                att = S.sb([128, 4, 512], F32, "att")
                attb = Buf("att")
                asq = S.sb([128, 2, 512], BF16, "asq")
                asqb = [Buf("asq0"), Buf("asq1")]
                ars = S.sb([128, 512], F32, "ars")
                arsb = Buf("ars")
                scale = 1.0 / float(np.sqrt(96.0))
                tiles_a = [ti for ti in range(len(TILES)) if not (ti == 0 and last)]
                npt_ = [0]

                def q_proj(ti, heads=range(8)):
                    t0, N = TILES[ti]
                    isctx = ti == 0
                    QT, QTb = QT2[ti % 2], QTb2[ti % 2]
                    if not isctx and 0 in heads:
                        dma("sp", ropeq[64:96, :, :N], rope_in[64:96, :, t0 - CTX:t0 - CTX + N], writes=[ropeqb])
                    for h in heads:
                        ba, bab = ps()
                        for kc in range(2):
                            op("pe", lambda e: e.matmul(ba[0:96, :N], lhsT=wq[:, kc, h, 0:96], rhs=qnT[:, kc, t0:t0 + N], start=(kc == 0), stop=(kc == 1)), reads=[wqb_, qnb], writes=[bab])
                        op("act", lambda e: e.activation(out=QT[0:64, h, :N], in_=ba[0:64, :N], func=AF.Copy), reads=[bab], writes=[QTb])
                        if isctx:
                            op("act", lambda e: e.activation(out=QT[64:96, h, :N], in_=ba[64:96, :N], func=AF.Copy), reads=[bab], writes=[QTb])
                        else:
                            bp, bpb = ps()
                            for kc in range(2):
                                op("pe", lambda e: e.matmul(bp[64:96, :N], lhsT=wq[:, kc, h, 96:128], rhs=qnT[:, kc, t0:t0 + N], start=(kc == 0), stop=(kc == 1)), reads=[wqb_, qnb], writes=[bpb])
                            op("dve", lambda e: e.tensor_tensor(out=qrt[64:96, 0, :N], in0=ba[64:96, :N], in1=ropeq[64:96, 0, :N], op=ALU.mult), reads=[bab, ropeqb], writes=[qrtb])
                            op("dve", lambda e: e.tensor_tensor(out=qrt[64:96, 1, :N], in0=bp[64:96, :N], in1=ropeq[64:96, 1, :N], op=ALU.mult), reads=[bpb, ropeqb], writes=[qrtb])
                            op("pool", lambda e: e.tensor_tensor(out=QT[64:96, h, :N], in0=qrt[64:96, 0, :N], in1=qrt[64:96, 1, :N], op=ALU.add), reads=[qrtb], writes=[QTb])

                def emit_s(ti, h, kp):
                    t0, N = TILES[ti]
                    QT, QTb = QT2[ti % 2], QTb2[ti % 2]
                    bi = npt_[0] % 2
                    big, bgb = bigs[bi], bigb[bi]
                    for j in range(2):
                        kt = 2 * kp + j
                        op("pe", lambda e: e.matmul(big[:, j * 512:j * 512 + N], lhsT=KT[0:96, h, kt * 128:(kt + 1) * 128], rhs=QT[0:96, h, :N], start=True, stop=True),
                           reads=[KTb, KTrb, QTb], writes=[bgb])
                    pi = npt_[0] % 3
                    npt_[0] += 1
                    op("act", lambda e: e.activation(out=PT[pi][:, :, :N], in_=big[:, :].rearrange("p (j n) -> p j n", j=2)[:, :, :N], func=AF.Exp, scale=scale), reads=[bgb], writes=[PTb[pi]])
                    return pi

                def att_norm(ti):
                    t0, N = TILES[ti]
                    if "att" in tap_out and b == 0 and l == 0 and ti == 1:
                        tap("att", att[:].rearrange("p c t -> p (c t)"), [attb])
                    bss, bssb = ps()
                    for c in range(4):
                        op("act", lambda e: e.activation(out=asq[:, c % 2, :N], in_=att[:, c, :N], func=AF.Square), reads=[attb], writes=[asqb[c % 2]])
                        op("pe", lambda e: e.matmul(bss[:, :N], lhsT=ones[:], rhs=asq[:, c % 2, :N], start=(c == 0), stop=(c == 3)), reads=[asqb[c % 2], constb], writes=[bssb])
                    rstd_from_ss(bss, bssb, N, 1.0 / 512, ars, arsb)
                    for c in range(4):
                        op("dve", lambda e: e.scalar_tensor_tensor(out=ya[:, c, t0:t0 + N], in0=att[:, c, :N], scalar=vec("out_norm", l, c), in1=ars[:, :N],
                                                                   op0=ALU.mult, op1=ALU.mult), reads=[attb, arsb, vecsb], writes=[yab])

                def emit_pv(ti, h, kp, pi):
                    t0, N = TILES[ti]
                    nkt = 2 if ti == 0 else 18
                    ou, oub = banks[5 + h % 2], bankb[5 + h % 2]
                    for j in range(2):
                        kt = 2 * kp + j
                        lhs = VA[:, kt, h // 2, 0:128] if h % 2 == 0 else VA[:, kt, h // 2, 64:192]
                        op("pe", lambda e: e.matmul(ou[:, :N], lhsT=lhs, rhs=PT[pi][:, j, :N], start=(kt == 0), stop=(kt == nkt - 1)), reads=[VAb, PTb[pi]], writes=[oub])
                    if 2 * kp + 1 == nkt - 1:
                        po = (h % 2) * 64
                        pd = 64 - po
                        op("dve", lambda e: e.reciprocal(out=Rr[po:po + 64, :N], in_=ou[pd:pd + 64, :N]), reads=[oub], writes=[Rrb])
                        op("dve", lambda e: e.tensor_tensor(out=att[po:po + 64, h // 2, :N], in0=ou[po:po + 64, :N], in1=Rr[po:po + 64, :N], op=ALU.mult), reads=[oub, Rrb], writes=[attb])
                        if h == 7:
                            att_norm(ti)

                items = []
                for k_, ti in enumerate(tiles_a):
                    nkt = 2 if ti == 0 else 18
                    for h in range(8):
                        for kp in range(nkt // 2):
                            items.append((ti, h, kp, k_))
                LOOK = DBG.get("look", 2)
                pend = []
                q_proj(tiles_a[0])
                for (ti, h, kt, k_) in items:
                    if kt == 0 and k_ + 1 < len(tiles_a):
                        q_proj(tiles_a[k_ + 1], [h])
                    pend.append((ti, h, kt, emit_s(ti, h, kt)))
                    if len(pend) > LOOK:
                        emit_pv(*pend.pop(0))
                while pend:
                    emit_pv(*pend.pop(0))
                pool_ids[:] = [0, 1, 2, 3, 4]
                S.pop()
            S.pop()
            if stage >= 4:
                S.push()
                xt = S.sb([128, 8, 512], F32, "xt2")
                xtb = Buf("xt2")
                sq = S.sb([128, 2, 512], BF16, "sq2")
                sqb = [Buf("sq0"), Buf("sq1")]
                rs = S.sb([128, 512], F32, "rs2")
                rsb = Buf("rs2")
                tmp = S.sb([128, 2, 512], F32, "tmp2")
                tmpb = [Buf("tmp0"), Buf("tmp1")]
                hT = S.sb([128, 8, 512], BF16, "hT2")
                hTb = Buf("hT2")
                h1 = S.sb([128, NJ, 512], BF16, "h1")
                h1b = Buf("h1")
                NWG, NW2 = 3, 2
                woa = S.sb([128, 8, 8, 128], BF16, "woa")
                woab = Buf("woa")
                wg = [S.sb([128, 2, 8, 256], BF16, "wg") for _ in range(NWG)]
                wgb = [Buf("wg%d" % i) for i in range(NWG)]
                w2 = [S.sb([128, 2, NJ, 128], BF16, "w2") for _ in range(NW2)]
                w2b = [Buf("w2%d" % i) for i in range(NW2)]
                sgs = S.sb([128, 2, 512], F32, "sgs")
                sgsb = [Buf("sgs0"), Buf("sgs1")]
                woutv = wview(l, "wout", "(m p k n) -> m p k n", m=8, p=128, k=8)
                wguv = wview(l, "wgu", "(j p k n) -> j p k n", j=NJ, p=128, k=8)
                wo2v = wview(l, "wo2", "(m p j n) -> m p j n", m=8, p=128, j=NJ)
                xt_2 = [xt, S.sb([128, 8, 512], F32, "xt3")]
                xtb_2 = [xtb, Buf("xt3")]
                cnts = dict(nwg=0, nw2=0, nsg=0)
                tiles_f = [ti for ti in range(len(TILES)) if not (ti == 0 and last)]

                def f_wout(ti):
                    t0, N = TILES[ti]
                    r = 2 if ti == 0 else b
                    xt, xtb = xt_2[ti % 2], xtb_2[ti % 2]
                    load_x(b, ti, l, xt, xtb)
                    dma("sp", woa[:], woutv.rearrange("m p k n -> p m k n"), reads=[wbuf[(l, "wout")]], writes=[woab])
                    for m in range(8):
                        bk, bb = ps()
                        for kc in range(8):
                            rhs = ya[:, kc, t0:t0 + N] if kc < 4 else yrs[:, kc - 4, t0:t0 + N]
                            op("pe", lambda e: e.matmul(bk[:, :N], lhsT=woa[:, m, kc, :], rhs=rhs, start=(kc == 0), stop=(kc == 7)), reads=[woab, yab, yrsb], writes=[bb])
                        op("dve", lambda e: e.scalar_tensor_tensor(out=xt[:, m, :N], in0=bk[:, :N], scalar=modv[:, l, 16 + m, r:r + 1], in1=xt[:, m, :N], op0=ALU.mult, op1=ALU.add),
                           reads=[bb, modb, xtb], writes=[xtb])
                    if "xmid" in tap_out and b == 0 and l == 0 and ti == 1:
                        tap("xmid", xt[:].rearrange("p c t -> p (c t)"), [xtb])

                def f_norm(ti):
                    t0, N = TILES[ti]
                    r = 2 if ti == 0 else b
                    adaln(l, 1, r, xt_2[ti % 2], xtb_2[ti % 2], N, hT, hTb, sq, sqb, rs, rsb, tmp, tmpb)

                def f_p1(ti):
                    t0, N = TILES[ti]
                    for j in range(NJ):
                        if j % 2 == 0:
                            wi = cnts["nwg"] % NWG
                            cnts["nwg"] += 1
                            dma("sp", wg[wi][:], wguv[j:j + 2].rearrange("j p k n -> p j k n"), reads=[wbuf[(l, "wgu")]], writes=[wgb[wi]])
                        jj = j % 2
                        bg, bgb = ps()
                        for kc in range(8):
                            op("pe", lambda e: e.matmul(bg[:, :N], lhsT=wg[wi][:, jj, kc, 0:128], rhs=hT[:, kc, :N], start=(kc == 0), stop=(kc == 7)), reads=[wgb[wi], hTb], writes=[bgb])
                        bu, bub = ps()
                        for kc in range(8):
                            op("pe", lambda e: e.matmul(bu[:, :N], lhsT=wg[wi][:, jj, kc, 128:256], rhs=hT[:, kc, :N], start=(kc == 0), stop=(kc == 7)), reads=[wgb[wi], hTb], writes=[bub])
                        si = cnts["nsg"] % 2
                        cnts["nsg"] += 1
                        op("act", lambda e: e.activation(out=sgs[:, si, :N], in_=bg[:, :N], func=AF.Silu), reads=[bgb], writes=[sgsb[si]])
                        op("dve", lambda e: e.tensor_tensor(out=h1[:, j, :N], in0=sgs[:, si, :N], in1=bu[:, :N], op=ALU.mult), reads=[sgsb[si], bub], writes=[h1b])

                def f_p2(ti, ms):
                    t0, N = TILES[ti]
                    r = 2 if ti == 0 else b
                    xt, xtb = xt_2[ti % 2], xtb_2[ti % 2]
                    for m in ms:
                        if m % 2 == 0:
                            cnts["w2i"] = cnts["nw2"] % NW2
                            cnts["nw2"] += 1
                            dma("sp", w2[cnts["w2i"]][:], wo2v[m:m + 2].rearrange("m p j n -> p m j n"), reads=[wbuf[(l, "wo2")]], writes=[w2b[cnts["w2i"]]])
                        wi = cnts["w2i"]
                        bk, bb = ps()
                        for j in range(NJ):
                            op("pe", lambda e: e.matmul(bk[:, :N], lhsT=w2[wi][:, m % 2, j, :], rhs=h1[:, j, :N], start=(j == 0), stop=(j == NJ - 1)), reads=[w2b[wi], h1b], writes=[bb])
                        op("dve", lambda e: e.scalar_tensor_tensor(out=xt[:, m, :N], in0=bk[:, :N], scalar=modv[:, l, 40 + m, r:r + 1], in1=xt[:, m, :N], op0=ALU.mult, op1=ALU.add),
                           reads=[bb, modb, xtb], writes=[xtb])

                def f_fin(ti):
                    t0, N = TILES[ti]
                    xt, xtb = xt_2[ti % 2], xtb_2[ti % 2]
                    if not last:
                        dma("pool", xs[b, :, :, t0:t0 + N], xt[:, :, :N], reads=[xtb], writes=[xbuf[(b, ti)]])
                    else:
                        bss, bssb = ps()
                        for c in range(8):
                            op("act", lambda e: e.activation(out=sq[:, c % 2, :N], in_=xt[:, c, :N], func=AF.Square), reads=[xtb], writes=[sqb[c % 2]])
                            op("pe", lambda e: e.matmul(bss[:, :N], lhsT=ones[:], rhs=sq[:, c % 2, :N], start=(c == 0), stop=(c == 7)), reads=[sqb[c % 2], constb], writes=[bssb])
                        rstd_from_ss(bss, bssb, N, 1.0 / D, rs, rsb)
                        for c in range(8):
                            op("dve", lambda e: e.scalar_tensor_tensor(out=xt[:, c, :N], in0=xt[:, c, :N], scalar=vec("final", None, c), in1=rs[:, :N], op0=ALU.mult, op1=ALU.mult),
                               reads=[xtb, rsb, vecsb], writes=[xtb])
                        dma("pool", outT[b, :, :, t0 - CTX:t0 - CTX + N], xt[:, :, :N], reads=[xtb], writes=[outbuf])

                f_wout(tiles_f[0])
                if stage >= 5:
                    f_norm(tiles_f[0])
                for k_, ti in enumerate(tiles_f):
                    nxt = tiles_f[k_ + 1] if k_ + 1 < len(tiles_f) else None
                    if stage >= 5:
                        f_p1(ti)
                    if nxt is not None:
                        f_wout(nxt)
                    if stage >= 5:
                        f_p2(ti, range(0, 4))
                        if nxt is not None:
                            f_norm(nxt)
                        f_p2(ti, range(4, 8))
                    f_fin(ti)
                S.pop()
            S.pop()
            niter[0] += 1
            if niter[0] >= DBG.get("iters", 99):
                break
        if niter[0] >= DBG.get("iters", 99):
            break
            if stage < 99 and not (stage >= 5):
                break
        if stage < 99:
            break
    S.finish()
    return nc, S


def _prep_inputs(inp):
    wl = [_layer_weights(inp, l) for l in range(DEPTH)]
    wada = np.concatenate([np.ascontiguousarray(inp["w_ada"][l].reshape(8, 128, 6144).transpose(1, 0, 2)).reshape(ADA_ROWS, 2048) for l in range(DEPTH)], 0)
    sgn = np.ascontiguousarray(np.broadcast_to(np.concatenate([inp["sgu_norm"][l] for l in range(DEPTH)])[None, :], (128, DEPTH * 256))).astype(np.float32)
    rope = _rope_tables()
    ident = np.eye(128, dtype=np.float32)
    maps = []
    for core in range(NCORES):
        bs = [2 * core, 2 * core + 1]
        xfull = np.concatenate([inp["ctx"][bs], inp["x"][bs]], axis=1)
        xT = np.ascontiguousarray(xfull.reshape(2, TOK, 8, 128).transpose(0, 3, 2, 1))
        vecs = _vecs(inp, [inp["c"][bs[0]], inp["c"][bs[1]], inp["c_ctx"]])
        maps.append({"xT": xT, "wl0": wl[0], "wl1": wl[1], "wada": wada, "vecs": vecs, "sgn": sgn, "rope": rope, "ident": ident})
    return maps


def kernel(**inputs):
    inp = {k: np.asarray(v, dtype=np.float32) for k, v in inputs.items()}
    _, S0 = build_program()
    nc, _ = build_program(marks=S0.used)
    maps = _prep_inputs(inp)
    res = run_bass_kernel_spmd(nc, maps, core_ids=list(range(NCORES)))
    out = np.empty((16, SEQ, D), np.float32)
    for core in range(NCORES):
        oT = res.results[core]["outT"]
        out[2 * core:2 * core + 2] = oT.transpose(0, 3, 2, 1).reshape(2, SEQ, D)
    return out
```
